# Optimizing a Trainium2 kernel written in Bass

```python
import math
import jax, jax.numpy as jnp
from jax import lax
import numpy as np

D_MODEL = 1024
BATCH = 4
SEQ = 4096
DEPTH = 4

MIX_WIDTH = D_MODEL
DA_WIDTH = MIX_WIDTH // 2
SC_WIDTH = MIX_WIDTH // 4
LRU_WIDTH = MIX_WIDTH - DA_WIDTH - SC_WIDTH

DA_HEAD_DIM = 64
DA_HEADS = DA_WIDTH // (2 * DA_HEAD_DIM)
Q_BLOCK = 128
NUM_BUCKETS = 32
MAX_DISTANCE = 128
SUBLN_EPS = 1e-5

SC_KERNEL = 3

LRU_BLOCKS = 4
LRU_BLOCK = LRU_WIDTH // LRU_BLOCKS
LRU_CONV = 4
LRU_C = 8.0

D_FF = 2816
RMS_EPS = 1e-6
NEG_INF = -1e30

IN_SPLIT_WIDTHS = (DA_WIDTH, DA_WIDTH, DA_WIDTH,
                   SC_WIDTH, SC_WIDTH, SC_WIDTH,
                   LRU_WIDTH, LRU_WIDTH)
IN_WIDTH = sum(IN_SPLIT_WIDTHS)
IN_SPLIT_POINTS = tuple(int(v) for v in np.cumsum(IN_SPLIT_WIDTHS)[:-1])

kernel_name = "hymba_diffattn_shortconv_rglru_macaron"


def rms_norm(x, g, eps=RMS_EPS):
    xf = x.astype(jnp.float32)
    y = xf * lax.rsqrt(jnp.mean(xf * xf, axis=-1, keepdims=True) + eps)
    return (y * g.astype(jnp.float32)).astype(x.dtype)


def swiglu(x, w_gate, w_up, w_down):
    return (jax.nn.silu(x @ w_gate) * (x @ w_up)) @ w_down


def causal_dwconv(x, w):
    K = w.shape[0]
    S = x.shape[1]
    xp = jnp.pad(x, ((0, 0), (K - 1, 0), (0, 0)))
    return sum(xp[:, k:k + S] * w[k] for k in range(K))


def t5_bucket(n):
    max_exact = NUM_BUCKETS // 2
    nf = jnp.maximum(n, 1).astype(jnp.float32)
    large = max_exact + (jnp.log(nf / max_exact) / math.log(MAX_DISTANCE / max_exact)
                         * (NUM_BUCKETS - max_exact)).astype(jnp.int32)
    large = jnp.minimum(large, NUM_BUCKETS - 1)
    return jnp.where(n < max_exact, n, large)


def diff_attention(q, k, v, lq1, lk1, lq2, lk2, subln_g, rel_bias, lam_init):
    B, S, _ = q.shape
    nblk = S // Q_BLOCK
    q = q.reshape(B, S, DA_HEADS, 2, DA_HEAD_DIM) * (DA_HEAD_DIM ** -0.5)
    k = k.reshape(B, S, DA_HEADS, 2, DA_HEAD_DIM)
    v = v.reshape(B, S, DA_HEADS, 2 * DA_HEAD_DIM)
    f32 = jnp.float32
    lam = (jnp.exp(jnp.sum(lq1.astype(f32) * lk1.astype(f32)))
           - jnp.exp(jnp.sum(lq2.astype(f32) * lk2.astype(f32))) + lam_init)
    k_pos = jnp.arange(S)
    q_blocks = jnp.moveaxis(q.reshape(B, nblk, Q_BLOCK, DA_HEADS, 2, DA_HEAD_DIM), 1, 0)

    def block(args):
        qb, start = args
        q_pos = start + jnp.arange(Q_BLOCK)
        dist = q_pos[:, None] - k_pos[None, :]
        bias = rel_bias[t5_bucket(jnp.maximum(dist, 0))]
        bias = bias.reshape(Q_BLOCK, S, DA_HEADS, 2).transpose(2, 3, 0, 1).astype(f32)
        logits = jnp.einsum('bqhmd,bkhmd->bhmqk', qb, k).astype(f32) + bias
        logits = jnp.where(dist >= 0, logits, NEG_INF)
        p = jax.nn.softmax(logits, axis=-1)
        attn = p[:, :, 0] - lam * p[:, :, 1]
        return jnp.einsum('bhqk,bkhe->bqhe', attn.astype(v.dtype), v)

    starts = jnp.arange(nblk) * Q_BLOCK
    o = lax.map(block, (q_blocks, starts))
    o = jnp.moveaxis(o, 0, 1).reshape(B, S, DA_HEADS, 2 * DA_HEAD_DIM)
    o = rms_norm(o, subln_g, SUBLN_EPS) * (1.0 - lam_init)
    return o.reshape(B, S, DA_WIDTH)


def block_diag_linear(x, w, b):
    B, S, _ = x.shape
    xb = x.reshape(B, S, LRU_BLOCKS, LRU_BLOCK)
    y = jnp.einsum('bsni,nij->bsnj', xb, w) + b
    return y.reshape(B, S, LRU_WIDTH)


def rg_lru(x, wa, ba, wx, bx, lam):
    f32 = jnp.float32
    r = jax.nn.sigmoid(block_diag_linear(x, wa, ba)).astype(f32)
    i = jax.nn.sigmoid(block_diag_linear(x, wx, bx)).astype(f32)
    log_a = -LRU_C * r * jax.nn.softplus(-lam.astype(f32))
    a = jnp.exp(log_a)
    mult = jnp.sqrt(-jnp.expm1(2.0 * log_a))
    b = mult * (i * x.astype(f32))

    def combine(e, l):
        return (e[0] * l[0], l[0] * e[1] + l[1])

    _, h = lax.associative_scan(combine, (a, b), axis=1)
    return h.astype(x.dtype)


def setup_inputs(seed: int = 0) -> dict:
    key = jax.random.key(seed)
    ks = iter(jax.random.split(key, 40))
    f32 = jnp.float32

    def nrm(shape, fan_in):
        return jax.random.normal(next(ks), shape, f32) * fan_in ** -0.5

    def gain(shape):
        return 1.0 + 0.02 * jax.random.normal(next(ks), shape, f32)

    def small(shape, s):
        return s * jax.random.normal(next(ks), shape, f32)

    u = jax.random.uniform(next(ks), (DEPTH, LRU_WIDTH), f32, minval=0.9, maxval=0.999)
    a0 = u ** (1.0 / LRU_C)
    lru_lambda = jnp.log(a0) - jnp.log1p(-a0)

    return {
        "x": jax.random.normal(next(ks), (BATCH, SEQ, D_MODEL), f32),
        "rel_bias": small((NUM_BUCKETS, 2 * DA_HEADS), 0.5),
        "ffn1_norm": gain((DEPTH, D_MODEL)),
        "ffn1_gate": nrm((DEPTH, D_MODEL, D_FF), D_MODEL),
        "ffn1_up": nrm((DEPTH, D_MODEL, D_FF), D_MODEL),
        "ffn1_down": nrm((DEPTH, D_FF, D_MODEL), D_FF),
        "mix_norm": gain((DEPTH, D_MODEL)),
        "w_in": nrm((DEPTH, D_MODEL, IN_WIDTH), D_MODEL),
        "w_out": nrm((DEPTH, MIX_WIDTH, D_MODEL), MIX_WIDTH),
        "lam_q1": small((DEPTH, DA_HEAD_DIM), 0.1),
        "lam_k1": small((DEPTH, DA_HEAD_DIM), 0.1),
        "lam_q2": small((DEPTH, DA_HEAD_DIM), 0.1),
        "lam_k2": small((DEPTH, DA_HEAD_DIM), 0.1),
        "subln_gain": gain((DEPTH, 2 * DA_HEAD_DIM)),
        "sc_conv_w": nrm((DEPTH, SC_KERNEL, SC_WIDTH), SC_KERNEL),
        "lru_conv_w": nrm((DEPTH, LRU_CONV, LRU_WIDTH), LRU_CONV),
        "lru_conv_b": small((DEPTH, LRU_WIDTH), 0.01),
        "lru_wa": nrm((DEPTH, LRU_BLOCKS, LRU_BLOCK, LRU_BLOCK), LRU_BLOCK),
        "lru_ba": small((DEPTH, LRU_BLOCKS, LRU_BLOCK), 0.01),
        "lru_wx": nrm((DEPTH, LRU_BLOCKS, LRU_BLOCK, LRU_BLOCK), LRU_BLOCK),
        "lru_bx": small((DEPTH, LRU_BLOCKS, LRU_BLOCK), 0.01),
        "lru_lambda": lru_lambda,
        "ffn2_norm": gain((DEPTH, D_MODEL)),
        "ffn2_gate": nrm((DEPTH, D_MODEL, D_FF), D_MODEL),
        "ffn2_up": nrm((DEPTH, D_MODEL, D_FF), D_MODEL),
        "ffn2_down": nrm((DEPTH, D_FF, D_MODEL), D_FF),
        "final_norm": gain((D_MODEL,)),
    }


def reference(x, rel_bias, ffn1_norm, ffn1_gate, ffn1_up, ffn1_down, mix_norm, w_in, w_out,
              lam_q1, lam_k1, lam_q2, lam_k2, subln_gain, sc_conv_w, lru_conv_w, lru_conv_b,
              lru_wa, lru_ba, lru_wx, lru_bx, lru_lambda, ffn2_norm, ffn2_gate, ffn2_up,
              ffn2_down, final_norm):
    for l in range(DEPTH):
        h = x + 0.5 * swiglu(rms_norm(x, ffn1_norm[l]), ffn1_gate[l], ffn1_up[l], ffn1_down[l])

        u = rms_norm(h, mix_norm[l])
        z = u @ w_in[l]
        q, k, v, sc_b, sc_c, sc_x, lru_x, lru_g = jnp.split(z, IN_SPLIT_POINTS, axis=-1)

        lam_init = 0.8 - 0.6 * math.exp(-0.3 * l)
        y_da = diff_attention(q, k, v, lam_q1[l], lam_k1[l], lam_q2[l], lam_k2[l],
                              subln_gain[l], rel_bias, lam_init)
        y_sc = sc_b * causal_dwconv(sc_c * sc_x, sc_conv_w[l])
        xr = causal_dwconv(lru_x, lru_conv_w[l]) + lru_conv_b[l]
        y_lru = jax.nn.gelu(lru_g) * rg_lru(xr, lru_wa[l], lru_ba[l], lru_wx[l], lru_bx[l],
                                            lru_lambda[l])

        h = h + jnp.concatenate([y_da, y_sc, y_lru], axis=-1) @ w_out[l]

        x = h + 0.5 * swiglu(rms_norm(h, ffn2_norm[l]), ffn2_gate[l], ffn2_up[l], ffn2_down[l])
    return rms_norm(x, final_norm)
```

```python
import contextlib
import math
import numpy as np
import concourse.bass as bass
import concourse.mybir as mybir
from concourse.bass_utils import run_bass_kernel_spmd

F32 = mybir.dt.float32
BF16 = mybir.dt.bfloat16
ALU = mybir.AluOpType
AF = mybir.ActivationFunctionType

PE, ACT, DVE, POOL, SP = "pe", "act", "dve", "pool", "sp"
ENGS = (PE, ACT, DVE, POOL, SP)
NDMASEM = 16

D = 1024
KC = 8
DFF = 2816
FC = 22
G = 512
IN_W = 2816
RMS_EPS = 1e-6
SUBLN_EPS = 1e-5
NBUCK = 32
N_CORES = 8
NFILL = 12


class Sched:
    def __init__(self, nc):
        self.nc = nc
        self.ops = {e: [] for e in ENGS}
        self.last_w = {}
        self.readers = {}
        self.dma_cnt = {e: 0 for e in ENGS}
        self.dma_ops = {e: [] for e in ENGS}
        self.label = ""

    def op(self, eng, fn, reads=(), writes=(), dma=False):
        deps = []
        for t in reads:
            w = self.last_w.get(t)
            if w is not None:
                deps.append((w, "raw"))
        for t in writes:
            w = self.last_w.get(t)
            if w is not None:
                deps.append((w, "waw"))
            for r in self.readers.get(t, ()):
                deps.append((r, "war"))
        seq = len(self.ops[eng])
        rec = dict(fn=fn, deps=deps, dma=dma, seq=seq, label=self.label)
        if dma:
            k = self.dma_cnt[eng]
            self.dma_cnt[eng] += 1
            rec["dma_k"] = k
            if k >= NDMASEM:
                deps.append(((eng, self.dma_ops[eng][k - NDMASEM]), "raw"))
            self.dma_ops[eng].append(seq)
        self.ops[eng].append(rec)
        ev = (eng, seq)
        for t in writes:
            self.last_w[t] = ev
            self.readers[t] = []
        for t in reads:
            self.readers.setdefault(t, []).append(ev)
        return ev

    def emit(self):
        nc = self.nc
        for e in ENGS:
            seen = {}
            seen_dma = set()
            for rec in self.ops[e]:
                waits = []
                best = {}
                for (f, s), kind in rec["deps"]:
                    if self.ops[f][s]["dma"]:
                        if (f, s) not in seen_dma:
                            seen_dma.add((f, s))
                            waits.append((f, s, True))
                    else:
                        if f == e and (e == PE or e == SP):
                            continue
                        if seen.get(f, -1) >= s:
                            continue
                        if best.get(f, -1) < s:
                            best[f] = s
                for f, s in best.items():
                    seen[f] = s
                    waits.append((f, s, False))
                rec["waits"] = waits
                rec["deps"] = None
        sig = {e: set() for e in ENGS}
        for e in ENGS:
            for rec in self.ops[e]:
                for (f, s, isdma) in rec["waits"]:
                    if not isdma:
                        sig[f].add(s)
        cnt = {}
        for e in ENGS:
            for i, s in enumerate(sorted(sig[e])):
                cnt[(e, s)] = i + 1
        self.stats = {e: (len(self.ops[e]), len(sig[e])) for e in ENGS}
        with contextlib.ExitStack() as st:
            esem = {e: st.enter_context(nc.semaphore("s_" + e)) for e in ENGS}
            dsem = {e: [st.enter_context(nc.semaphore("d_%s%d" % (e, j))) for j in range(NDMASEM)]
                    for e in ENGS if self.dma_cnt[e] > 0}
            block = st.enter_context(nc.Block())

            def body(e):
                def run(engobj):
                    for rec in self.ops[e]:
                        for (f, s, isdma) in rec["waits"]:
                            if isdma:
                                k = self.ops[f][s]["dma_k"]
                                engobj.wait_ge(dsem[f][k % NDMASEM], 16 * (k // NDMASEM + 1))
                            else:
                                engobj.wait_ge(esem[f], cnt[(f, s)])
                        if rec["fn"] is None:
                            continue
                        ins = rec["fn"](engobj)
                        if rec["dma"]:
                            k = rec["dma_k"]
                            ins.then_inc(dsem[e][k % NDMASEM], 16)
                        elif (e, rec["seq"]) in cnt:
                            ins.then_inc(esem[e], 1)
                return run

            if self.ops[PE]:
                block.tensor(body(PE))
            if self.ops[ACT]:
                block.scalar(body(ACT))
            if self.ops[DVE]:
                block.vector(body(DVE))
            if self.ops[POOL]:
                block.gpsimd(body(POOL))
            if self.ops[SP]:
                block.sync(body(SP))


def _vcols(nl):
    o = {}
    c = 0
    for nm, n in (("n1", nl * 8), ("nm", nl * 8), ("n2", nl * 8), ("nf", 8),
                  ("scw", nl * 6), ("lcw", nl * 8), ("lcb", nl * 2), ("lba", nl * 2),
                  ("lbx", nl * 2), ("llam", nl * 2), ("subg", nl)):
        o[nm] = c
        c += n
    o["_n"] = c
    return o


def _lam_init(l):
    return 0.8 - 0.6 * math.exp(-0.3 * l)


def build_nc(nl=4, seq=4096):
    NG = seq // G
    NT = seq // 128
    VC = _vcols(nl)
    nc = bass.Bass("TRN2", target_bir_lowering=False)

    def din(name, shape, dt=F32):
        return nc.dram_tensor(name, list(shape), dt, kind="ExternalInput").ap()

    x_in = din("x", [seq, D])
    wg = [din("g1", [nl, D, DFF]), din("g2", [nl, D, DFF])]
    wu = [din("u1", [nl, D, DFF]), din("u2", [nl, D, DFF])]
    wd = [din("d1", [nl, DFF, D]), din("d2", [nl, DFF, D])]
    w_in = din("win", [nl, D, IN_W])
    w_out = din("wout", [nl, D, D])
    vecs_in = din("vecs", [128, VC["_n"]])
    lamv_in = din("lamv", [64, nl * 4])
    relb_in = din("relb", [NBUCK, 8])
    mb_in = din("mb", [NBUCK, 384])
    ident_in = din("ident", [128, 128])
    wbd_in = din("wbd", [128, nl * 4, 128])
    out = nc.dram_tensor("out", [seq, D], F32, kind="ExternalOutput").ap()

    def dscr(name, shape, dt):
        return nc.dram_tensor(name, list(shape), dt, kind="Internal").ap()

    gu_s = [dscr("gu_s%d" % f, [nl, FC, 128, 2 * KC * 128], BF16) for f in range(2)]
    dd_s = [dscr("dd_s%d" % f, [nl, 2, 128, FC, 512], BF16) for f in range(2)]
    wi_s = dscr("wi_s", [nl, 9, 128, 2 * KC * 128], BF16)
    wv_s = dscr("wv_s", [nl, 2, 128, KC * 256], BF16)
    wo_s = dscr("wo_s", [nl, 4, 128, 2 * KC * 128], BF16)
    xs_s = dscr("xs_s", [NG, 128, KC, G], F32)
    tz_t = nc.dram_tensor("tz_s", [8, 128 * 385 + 512], F32, kind="Internal")

    S = Sched(nc)
    with contextlib.ExitStack() as st:
        def sb(name, shape, dt):
            return st.enter_context(nc.sbuf_tensor(name + "_sb", list(shape), dt))

        kT = sb("kT", [128, 4, seq], BF16)
        vS = sb("vS", [128, NT, 512], BF16)
        xg = sb("xg", [128, KC, G], F32)
        u = sb("u", [128, KC, G], BF16)
        hid = sb("hid", [128, FC, G], BF16)
        wr = [sb("wr%d" % i, [128, 2048], BF16) for i in range(4)]
        qT = sb("qT", [128, 4, G], BF16)
        mix = sb("mix", [128, KC, G], BF16)
        tmp = sb("tmp", [128, 15, G], F32)
        pbuf = sb("pbuf", [128, 2, 2 + G], F32)
        lbuf = sb("lbuf", [128, 2, 3 + G], F32)
        hst = sb("hst", [128, 2], F32)
        xrb = sb("xrb", [128, 2, G], BF16)
        sqo = sb("sqo", [128, G], BF16)
        pT = sb("pT", [128, 2, 2 * G], BF16)
        vecs = sb("vecs", [128, VC["_n"]], F32)
        dv = sb("dv", [128, 64], F32)
        ident = sb("ident", [128, 128], F32)
        ones_mean = sb("ones_mean", [128, 128], BF16)
        ones_e = sb("ones_e", [128, 128], BF16)
        ones_k = sb("ones_k", [128, 128], BF16)
        ones_f = sb("ones_f", [128, 128], F32)
        wbd = sb("wbd", [128, nl * 4, 128], BF16)
        Bt = sb("Bt", [128, 8, 2, 128], F32)
        relb = sb("relb", [NBUCK, 8], F32)
        mb = sb("mb", [NBUCK, 384], F32)
        rrep = sb("rrep", [NBUCK, 128], F32)
        fb = sb("fb", [128, 384], F32)
        lamv = sb("lamv", [64, nl * 4], F32)
        lprod = sb("lprod", [64, 2, nl], F32)
        psb = [st.enter_context(nc.psum_tensor("psb%d" % b, [128, 1024], F32)) for b in range(4)]
        ps = [psb[b // 2][:, (b % 2) * 512:(b % 2 + 1) * 512] for b in range(8)]
        wbd_f = tmp[:, 0:nl, :].rearrange("p a (b c) -> p (a b) c", c=128)
        WBT = [("tmp", i) for i in range(nl)]
        io = [tmp[:, 6 + 2 * bi:8 + 2 * bi, :].rearrange("p a b -> p (a b)") for bi in range(2)]
        IOT = [[("tmp", 6 + 2 * bi), ("tmp", 7 + 2 * bi)] for bi in range(2)]

        DV_CNEG, DV_CNEG2, DV_LAM, DV_NLAM, DV_GSUB, DV_E = 0, 8, 16, 20, 24, 32

        def PS(b):
            return ("ps", b)

        def dma(eng, out_ap, in_ap, reads, writes):
            S.op(eng, lambda e: e.dma_start(out=out_ap, in_=in_ap), reads, writes, dma=True)

        def mm(out_ap, lhsT, rhs, start, stop, reads, writes):
            S.op(PE, lambda e: e.matmul(out_ap, lhsT, rhs, start=start, stop=stop), reads, writes)

        def act(out_ap, in_ap, func, reads, writes, bias=None, scale=None):
            kw = {}
            if bias is not None:
                kw["bias"] = bias
            if scale is not None:
                kw["scale"] = scale
            S.op(ACT, lambda e: e.activation(out_ap, in_ap, func, **kw), reads, writes)

        def tt(out_ap, a, b, op, reads, writes, eng=DVE):
            S.op(eng, lambda e: e.tensor_tensor(out_ap, a, b, op), reads, writes)

        def ts(out_ap, a, s1, s2, op0, op1, reads, writes, eng=DVE):
            if op1 is None:
                S.op(eng, lambda e: e.tensor_scalar(out_ap, a, s1, None, op0), reads, writes)
            else:
                S.op(eng, lambda e: e.tensor_scalar(out_ap, a, s1, s2, op0, op1), reads, writes)

        def stt(out_ap, a, scalar, b, op0, op1, reads, writes):
            S.op(DVE, lambda e: e.scalar_tensor_tensor(out_ap, a, scalar, b, op0, op1), reads, writes)

        def cp(eng, out_ap, in_ap, reads, writes):
            if eng == ACT:
                act(out_ap, in_ap, AF.Copy, reads, writes)
            else:
                S.op(eng, lambda e: e.tensor_copy(out_ap, in_ap), reads, writes)

        def recip(out_ap, in_ap, reads, writes):
            S.op(DVE, lambda e: e.reciprocal(out_ap, in_ap), reads, writes)

        def memset(eng, ap, val, writes):
            S.op(eng, lambda e: e.memset(ap, val), (), writes)

        def cast_cols(dst, src2d, col0, wtoken):
            dma(POOL, dst.rearrange("p (k j) -> p k j", k=KC),
                src2d[:, col0:col0 + 128].rearrange("(k p) j -> p k j", p=128), (), [wtoken])

        def cast_ffn(l, f):
            for c in range(FC):
                cast_cols(gu_s[f][l, c, :, 0:KC * 128], wg[f][l], c * 128, ("gu", f, l, c, 0))
                cast_cols(gu_s[f][l, c, :, KC * 128:2 * KC * 128], wu[f][l], c * 128, ("gu", f, l, c, 1))
            for hh in range(2):
                for cb in range(0, FC, 4):
                    nb = min(4, FC - cb)
                    dma(POOL, dd_s[f][l, hh, :, cb:cb + nb, :],
                        wd[f][l][cb * 128:(cb + nb) * 128, hh * 512:(hh + 1) * 512].rearrange(
                            "(c p) j -> p c j", p=128), (), [("dd", f, l, hh, cb)])

        FM_COLS = [0, 1, 2, 3, 4, 5, 6, 7, 12, 13, 14, 15, 16, 17, 18, 19, 20, 21]

        def cast_mix(l):
            for pi in range(9):
                for t in range(2):
                    cast_cols(wi_s[l, pi, :, t * 1024:(t + 1) * 1024], w_in[l],
                              FM_COLS[pi * 2 + t] * 128, ("wi", l, pi, t))
            for hv in range(2):
                dma(POOL, wv_s[l, hv].rearrange("p (k j) -> p k j", k=KC),
                    w_in[l][:, 1024 + hv * 256:1024 + (hv + 1) * 256].rearrange("(k p) j -> p k j", p=128),
                    (), [("wv", l, hv)])
            for pj in range(4):
                for t in range(2):
                    cast_cols(wo_s[l, pj, :, t * 1024:(t + 1) * 1024], w_out[l],
                              (pj * 2 + t) * 128, ("wo", l, pj, t))

        for l in range(nl):
            cast_ffn(l, 0)
            cast_mix(l)
            cast_ffn(l, 1)

        dma(SP, vecs[:], vecs_in, (), ["vecs"])
        dma(SP, ident[:], ident_in, (), ["ident"])
        dma(SP, relb[:], relb_in, (), ["relb"])
        dma(SP, mb[:], mb_in, (), ["mb"])
        dma(SP, lamv[:], lamv_in, (), ["lamv"])
        dma(SP, wbd_f, wbd_in, (), WBT)
        memset(DVE, ones_mean[:], 1.0 / D, ["ones_mean"])
        memset(DVE, ones_e[:], 1.0 / 128, ["ones_e"])
        memset(DVE, ones_k[:], 1.0, ["ones_k"])
        memset(DVE, ones_f[:], 1.0, ["ones_f"])
        memset(DVE, fb[:], -1e30, ["fb"])
        memset(DVE, pbuf[:], 0.0, ["pbuf0", "pbuf1"])
        memset(DVE, lbuf[:], 0.0, ["lbuf0", "lbuf1"])
        memset(DVE, hst[:], 0.0, ["hst0", "hst1"])
        cp(DVE, wbd[:], wbd_f, WBT, ["wbd"])

        lv = lamv[:].rearrange("p (l f) -> p l f", f=4)
        tt(lprod[:, 0, :], lv[:, :, 0], lv[:, :, 1], ALU.mult, ["lamv"], ["lprod"])
        tt(lprod[:, 1, :], lv[:, :, 2], lv[:, :, 3], ALU.mult, ["lamv", "lprod"], ["lprod"])
        mm(ps[0][:, 0:2 * nl], ones_f[0:64, :], lprod[:].rearrange("p a l -> p (a l)"), True, True,
           ["ones_f", "lprod"], [PS(0)])
        act(dv[:, DV_E:DV_E + 2 * nl], ps[0][:, 0:2 * nl], AF.Exp, [PS(0)], ["dv_e"])
        tt(dv[:, DV_LAM:DV_LAM + nl], dv[:, DV_E:DV_E + nl], dv[:, DV_E + nl:DV_E + 2 * nl], ALU.subtract,
           ["dv_e"], ["dv_lam"])
        for l in range(nl):
            ts(dv[:, DV_LAM + l:DV_LAM + l + 1], dv[:, DV_LAM + l:DV_LAM + l + 1], _lam_init(l), None, ALU.add, None,
               ["dv_lam"], ["dv_lam"])
            ts(dv[:, DV_GSUB + l:DV_GSUB + l + 1], vecs[:, VC["subg"] + l:VC["subg"] + l + 1], 1.0 - _lam_init(l), None,
               ALU.mult, None, ["vecs"], ["dv_gsub"])
        ts(dv[:, DV_NLAM:DV_NLAM + nl], dv[:, DV_LAM:DV_LAM + nl], -1.0, None, ALU.mult, None, ["dv_lam"], ["dv_nlam"])
        act(dv[:, DV_CNEG:DV_CNEG + 2 * nl], vecs[:, VC["llam"]:VC["llam"] + 2 * nl], AF.Exp, ["vecs"], ["dv_c"], scale=-1.0)
        act(dv[:, DV_CNEG:DV_CNEG + 2 * nl], dv[:, DV_CNEG:DV_CNEG + 2 * nl], AF.Ln, ["dv_c"], ["dv_c"], bias=1.0)
        ts(dv[:, DV_CNEG2:DV_CNEG2 + 2 * nl], dv[:, DV_CNEG:DV_CNEG + 2 * nl], -16.0, None, ALU.mult, None, ["dv_c"], ["dv_c2"])
        ts(dv[:, DV_CNEG:DV_CNEG + 2 * nl], dv[:, DV_CNEG:DV_CNEG + 2 * nl], -8.0, None, ALU.mult, None, ["dv_c", "dv_c2"], ["dv_c"])

        tz_w = [bass.AP(tz_t, hm * (128 * 385 + 512), [[385, 128], [1, 384]]) for hm in range(8)]
        for hm in range(8):
            ts(rrep[:], ones_f[0:NBUCK, :], relb[:, hm:hm + 1], None, ALU.mult, None, ["ones_f", "relb"], ["rrep"])
            mm(ps[1][:, 0:384], rrep[:], mb[:], True, True, ["rrep", "mb"], [PS(1)])
            cp(DVE, fb[:, 128:384], ps[1][:, 128:384], [PS(1)], ["fb"])
            dma(SP, tz_w[hm], fb[:], ["fb"], [("tz", hm)])
            for t, a in enumerate((128, 256)):
                dma(SP, Bt[:, hm, t, :], bass.AP(tz_t, hm * (128 * 385 + 512) + a, [[384, 128], [1, 128]]),
                    [("tz", hm)], [("Bt", hm, t)])

        XG = [("xg", k) for k in range(KC)]
        UU = [("u", k) for k in range(KC)]
        MIX = [("mix", k) for k in range(KC)]
        wslot = [0]

        deferred = []

        def wload(src_ap, rtokens, ncols=2048):
            s = wslot[0] % 4
            wslot[0] += 1
            dma(SP, wr[s][:, 0:ncols], src_ap, rtokens, [("w", s)])
            for item in list(deferred):
                item[0] -= 1
                if item[0] <= 0:
                    deferred.remove(item)
                    for fn in item[1]:
                        fn()
            return s

        def rstd_from(bank, ti, eps):
            rs = tmp[:, ti, :]
            act(rs, ps[bank], AF.Ln, [PS(bank)], [("tmp", ti)], bias=eps)
            act(rs, rs, AF.Exp, [("tmp", ti)], [("tmp", ti)], scale=-0.5)
            return rs

        def norm_to_u(col0, stat_bank, rstd_i):
            for k in range(KC):
                if k % 2 == 0:
                    act(mix[:, k, :], xg[:, k, :], AF.Square, [("xg", k)], [("mix", k)])
                else:
                    tt(mix[:, k, :], xg[:, k, :], xg[:, k, :], ALU.mult, [("xg", k)], [("mix", k)])
                mm(ps[stat_bank], ones_mean[:], mix[:, k, :], k == 0, k == KC - 1, ["ones_mean", ("mix", k)], [PS(stat_bank)])
            rs = rstd_from(stat_bank, rstd_i, RMS_EPS)
            for i in range(NFILL):
                mm(ps[3], ones_mean[:], mix[:, KC - 1, :], True, True, ["ones_mean", ("mix", KC - 1)], [PS(3)])
            for k in range(KC):
                stt(u[:, k, :], xg[:, k, :], vecs[:, col0 + k:col0 + k + 1], rs, ALU.mult, ALU.mult,
                    [("xg", k), "vecs", ("tmp", rstd_i)], [("u", k)])

        def ffn(l, f, on_evac=None):
            lab0 = S.label
            S.label = lab0 + ".norm"
            norm_to_u(VC["n1" if f == 0 else "n2"] + l * 8, 0, 0)
            S.label = lab0 + ".A"
            for c in range(FC):
                s = wload(gu_s[f][l, c], [("gu", f, l, c, 0), ("gu", f, l, c, 1)])
                bg, bu = (0, 1) if c % 2 == 0 else (2, 3)
                for k in range(KC):
                    mm(ps[bg], wr[s][:, k * 128:(k + 1) * 128], u[:, k, :], k == 0, k == KC - 1,
                       [("w", s), ("u", k)], [PS(bg)])
                for k in range(KC):
                    mm(ps[bu], wr[s][:, 1024 + k * 128:1024 + (k + 1) * 128], u[:, k, :], k == 0, k == KC - 1,
                       [("w", s), ("u", k)], [PS(bu)])
                ti = 1 + (c % 2)
                act(tmp[:, ti, :], ps[bg], AF.Silu, [PS(bg)], [("tmp", ti)])
                tt(hid[:, c, :], tmp[:, ti, :], ps[bu], ALU.mult, [("tmp", ti), PS(bu)], [("hid", c)])
            S.label = lab0 + ".B"
            for hh in range(2):
                for cb in range(0, FC, 4):
                    nb = min(4, FC - cb)
                    s = wload(dd_s[f][l, hh, :, cb:cb + nb, :].rearrange("p c j -> p (c j)"),
                              [("dd", f, l, hh, cb)], ncols=nb * 512)
                    for ci in range(nb):
                        c = cb + ci
                        for j in range(4):
                            mm(ps[4 + j], wr[s][:, ci * 512 + j * 128:ci * 512 + (j + 1) * 128], hid[:, c, :],
                               c == 0, c == FC - 1, [("w", s), ("hid", c)], [PS(4 + j)])
                for j in range(4):
                    k = hh * 4 + j
                    stt(xg[:, k, :], ps[4 + j], 0.5, xg[:, k, :], ALU.mult, ALU.add, [PS(4 + j), ("xg", k)], [("xg", k)])
                    if on_evac is not None:
                        on_evac(k, hh)

        def proj_pair(l, pi, banks):
            s = wload(wi_s[l, pi], [("wi", l, pi, 0), ("wi", l, pi, 1)])
            for t in range(2):
                b = banks[t]
                for k in range(KC):
                    mm(ps[b], wr[s][:, t * 1024 + k * 128:t * 1024 + (k + 1) * 128], u[:, k, :], k == 0, k == KC - 1,
                       [("w", s), ("u", k)], [PS(b)])

        def vcol(name, idx):
            c = VC[name] + idx
            return vecs[:, c:c + 1]

        def mixer(l, g):
            S.label = "mix.norm"
            norm_to_u(VC["nm"] + l * 8, 0, 0)
            t0 = g * G
            T = lambda i: tmp[:, i, :]
            TK = lambda i: ("tmp", i)
            LB = lambda cc: "lbuf%d" % cc
            XR = (6, 0)
            RR = (7, 1)
            II = (8, 2)
            AA = (9, 3)
            GG = (4, 5)

            S.label = "mix.lru"
            proj_pair(l, 7, (2, 3))
            for cc in range(2):
                cp(ACT, lbuf[:, cc, 3:3 + G], ps[2 + cc], [PS(2 + cc)], [LB(cc)])
            S.label = "mix.qkv"
            for pi in range(2):
                bk = (0, 1) if pi % 2 == 0 else (2, 3)
                proj_pair(l, pi, bk)
                for t in range(2):
                    h = pi * 2 + t
                    act(qT[:, h, :], ps[bk[t]], AF.Copy, [PS(bk[t])], [("qT", h)], scale=0.125)
            S.label = "mix.lru"
            for cc in range(2):
                wc = lambda k: vcol("lcw", (l * 4 + k) * 2 + cc)
                ts(T(XR[cc]), lbuf[:, cc, 3:3 + G], wc(3), vcol("lcb", l * 2 + cc), ALU.mult, ALU.add,
                   [LB(cc), "vecs"], [TK(XR[cc])])
                for k in (2, 1, 0):
                    stt(T(XR[cc]), lbuf[:, cc, k:k + G], wc(k), T(XR[cc]), ALU.mult, ALU.add,
                        [LB(cc), "vecs", TK(XR[cc])], [TK(XR[cc])])
                cp(DVE, lbuf[:, cc, 0:3], lbuf[:, cc, G:G + 3], [LB(cc)], [LB(cc)])
                cp(ACT, xrb[:, cc, :], T(XR[cc]), [TK(XR[cc])], [("xrb", cc)])
            S.label = "mix.qkv"
            for pi in range(2, 4):
                bk = (0, 1) if pi % 2 == 0 else (2, 3)
                proj_pair(l, pi, bk)
                for t in range(2):
                    h = (pi - 2) * 2 + t
                    cp(DVE, kT[:, h, t0:t0 + G], ps[bk[t]], [PS(bk[t])], [("kT", h, g)])
            for hv in range(2):
                s = wload(wv_s[l, hv], [("wv", l, hv)])
                for tt_ in range(4):
                    b = 4 + ((hv * 4 + tt_) % 4)
                    for k in range(KC):
                        mm(ps[b][:, 0:256], u[:, k, tt_ * 128:(tt_ + 1) * 128], wr[s][:, k * 256:(k + 1) * 256],
                           k == 0, k == KC - 1, [("w", s), ("u", k)], [PS(b)])
                    cp(ACT if tt_ % 2 == 0 else DVE, vS[:, g * 4 + tt_, hv * 256:(hv + 1) * 256], ps[b][:, 0:256],
                       [PS(b)], [("vS", g * 4 + tt_)])
            S.label = "mix.lru"
            for cc in range(2):
                ba, bx = (0, 1) if cc == 0 else (2, 3)
                mm(ps[ba], wbd[:, (l * 2 + 0) * 2 + cc, :], xrb[:, cc, :], True, True, ["wbd", ("xrb", cc)], [PS(ba)])
                mm(ps[bx], wbd[:, (l * 2 + 1) * 2 + cc, :], xrb[:, cc, :], True, True, ["wbd", ("xrb", cc)], [PS(bx)])
            S.label = "mix.sc"
            proj_pair(l, 4, (4, 5))
            for cc in range(2):
                cp(ACT, T(10 + cc), ps[4 + cc], [PS(4 + cc)], [TK(10 + cc)])
            proj_pair(l, 5, (6, 7))
            for cc in range(2):
                cp(ACT, T(12 + cc), ps[6 + cc], [PS(6 + cc)], [TK(12 + cc)])
            proj_pair(l, 6, (4, 5))
            for cc in range(2):
                pb = "pbuf%d" % cc
                tt(pbuf[:, cc, 2:2 + G], T(12 + cc), ps[4 + cc], ALU.mult, [TK(12 + cc), PS(4 + cc)], [pb])
            S.label = "mix.lru"
            for cc in range(2):
                ba, bx = (0, 1) if cc == 0 else (2, 3)
                act(T(RR[cc]), ps[ba], AF.Sigmoid, [PS(ba), "vecs"], [TK(RR[cc])], bias=vcol("lba", l * 2 + cc))
                act(T(II[cc]), ps[bx], AF.Sigmoid, [PS(bx), "vecs"], [TK(II[cc])], bias=vcol("lbx", l * 2 + cc))
            proj_pair(l, 8, (6, 7))
            for cc in range(2):
                cp(ACT, T(GG[cc]), ps[6 + cc], [PS(6 + cc)], [TK(GG[cc])])
            S.label = "mix.sc"
            for cc in range(2):
                pb = "pbuf%d" % cc
                acc = T(12 + cc)
                wc = lambda k: vcol("scw", (l * 3 + k) * 2 + cc)
                ts(acc, pbuf[:, cc, 2:2 + G], wc(2), None, ALU.mult, None, [pb, "vecs"], [TK(12 + cc)])
                stt(acc, pbuf[:, cc, 1:1 + G], wc(1), acc, ALU.mult, ALU.add, [pb, "vecs", TK(12 + cc)], [TK(12 + cc)])
                stt(acc, pbuf[:, cc, 0:G], wc(0), acc, ALU.mult, ALU.add, [pb, "vecs", TK(12 + cc)], [TK(12 + cc)])
                tt(mix[:, 4 + cc, :], acc, T(10 + cc), ALU.mult, [TK(12 + cc), TK(10 + cc)], [("mix", 4 + cc)])
                cp(DVE, pbuf[:, cc, 0:2], pbuf[:, cc, G:G + 2], [pb], [pb])
            S.label = "mix.lru"
            for cc in range(2):
                gi = GG[cc]
                sl = 14
                tt(T(sl), T(gi), T(gi), ALU.mult, [TK(gi)], [TK(sl)])
                ts(T(sl), T(sl), 0.044715, 1.0, ALU.mult, ALU.add, [TK(sl)], [TK(sl)])
                tt(T(sl), T(sl), T(gi), ALU.mult, [TK(sl), TK(gi)], [TK(sl)])
                act(T(sl), T(sl), AF.Sigmoid, [TK(sl)], [TK(sl)], scale=2.0 * math.sqrt(2.0 / math.pi))
                tt(T(gi), T(sl), T(gi), ALU.mult, [TK(sl), TK(gi)], [TK(gi)])

            def lru_bg(cc):
                ci = l * 2 + cc
                hs = "hst%d" % cc
                xr, rr, ii, aa, gi = XR[cc], RR[cc], II[cc], AA[cc], GG[cc]
                st = []
                st.append(lambda: act(T(aa), T(rr), AF.Exp, [TK(rr), "dv_c"], [TK(aa)],
                                      scale=dv[:, DV_CNEG + ci:DV_CNEG + ci + 1]))
                st.append(lambda: act(T(rr), T(rr), AF.Exp, [TK(rr), "dv_c2"], [TK(rr)],
                                      scale=dv[:, DV_CNEG2 + ci:DV_CNEG2 + ci + 1]))
                st.append(lambda: tt(T(ii), T(ii), T(xr), ALU.mult, [TK(ii), TK(xr)], [TK(ii)]))
                st.append(lambda: ts(T(rr), T(rr), -1.0, 1.0, ALU.mult, ALU.add, [TK(rr)], [TK(rr)]))
                st.append(lambda: ts(T(rr), T(rr), 1e-18, None, ALU.max, None, [TK(rr)], [TK(rr)]))
                st.append(lambda: act(T(rr), T(rr), AF.Ln, [TK(rr)], [TK(rr)]))
                st.append(lambda: act(T(rr), T(rr), AF.Exp, [TK(rr)], [TK(rr)], scale=0.5))
                st.append(lambda: tt(T(ii), T(ii), T(rr), ALU.mult, [TK(ii), TK(rr)], [TK(ii)]))
                st.append(lambda: S.op(DVE, lambda e: e.tensor_tensor_scan(tmp[:, xr, :], tmp[:, aa, :], tmp[:, ii, :],
                                                                          hst[:, cc:cc + 1], ALU.mult, ALU.add),
                                       [TK(aa), TK(ii), hs], [TK(xr)]))
                st.append(lambda: cp(DVE, hst[:, cc:cc + 1], tmp[:, xr, G - 1:G], [TK(xr)], [hs]))
                st.append(lambda: tt(mix[:, 6 + cc, :], T(gi), T(xr), ALU.mult, [TK(gi), TK(xr)], [("mix", 6 + cc)]))
                return st

            lruq = []
            for a_, b_ in zip(lru_bg(0), lru_bg(1)):
                lruq.append(a_)
                lruq.append(b_)
            finq = []

            def bg_step():
                lab = S.label
                if finq:
                    S.label = "mix.attnfin"
                    finq.pop(0)()
                elif lruq:
                    S.label = "mix.lru"
                    lruq.pop(0)()
                S.label = lab

            nkt = 4 * g + 4

            def fin_stages(h):
                def st0():
                    act(T(10), ps[5], AF.Ln, [PS(5)], [TK(10)])
                    cp(DVE, T(12), ps[4], [PS(4)], [TK(12)])
                    act(T(11), ps[7], AF.Ln, [PS(7)], [TK(11)])
                    cp(DVE, T(13), ps[6], [PS(6)], [TK(13)])

                def st1():
                    act(T(10), T(10), AF.Exp, [TK(10)], [TK(10)], scale=-1.0)
                    act(T(11), T(11), AF.Exp, [TK(11)], [TK(11)], scale=-1.0)

                def st2():
                    tt(T(12), T(12), T(10), ALU.mult, [TK(12), TK(10)], [TK(12)])
                    tt(T(13), T(13), T(11), ALU.mult, [TK(13), TK(11)], [TK(13)])
                    stt(T(12), T(13), dv[:, DV_NLAM + l:DV_NLAM + l + 1], T(12), ALU.mult, ALU.add,
                        [TK(13), TK(12), "dv_nlam"], [TK(12)])

                def st3():
                    act(sqo[:], T(12), AF.Square, [TK(12)], ["sqo"])

                def st4():
                    mm(ps[0], ones_e[:], sqo[:], True, True, ["ones_e", "sqo"], [PS(0)])
                    rstd_from(0, 14, SUBLN_EPS)

                def st5():
                    stt(mix[:, h, :], T(12), dv[:, DV_GSUB + l:DV_GSUB + l + 1], T(14), ALU.mult, ALU.mult,
                        [TK(12), TK(14), "dv_gsub"], [("mix", h)])

                return [st0, st1, st2, st3, st4, st5]

            for h in range(4):
                S.label = "mix.attn"

                def emitS(kt):
                    j = kt - 4 * g
                    c0 = 0 if j < 0 else 128 * j
                    buf = kt % 2
                    for m in range(2):
                        hm = h * 2 + m
                        b = buf * 2 + m
                        mm(ps[b][:, c0:G], kT[m * 64:(m + 1) * 64, h, kt * 128:(kt + 1) * 128], qT[m * 64:(m + 1) * 64, h, c0:G],
                           True, True, [("kT", h, kt // 4), ("qT", h)], [PS(b)])
                    pv = psb[buf][:].rearrange("p (m q) -> p m q", m=2)
                    BTK = [("Bt", h * 2 + m, t) for m in range(2) for t in range(2)]
                    PSK = [PS(buf * 2), PS(buf * 2 + 1)]
                    if j == -1:
                        tt(pv[:, :, 0:128], pv[:, :, 0:128], Bt[:, 2 * h:2 * h + 2, 1, :], ALU.add, PSK + BTK, PSK)
                    elif j == 3:
                        tt(pv[:, :, c0:c0 + 128], pv[:, :, c0:c0 + 128], Bt[:, 2 * h:2 * h + 2, 0, :], ALU.add, PSK + BTK, PSK)
                    elif j >= 0:
                        tt(pv[:, :, c0:c0 + 256], pv[:, :, c0:c0 + 256],
                           Bt[:, 2 * h:2 * h + 2, :, :].rearrange("p m t q -> p m (t q)"), ALU.add, PSK + BTK, PSK)

                def emitE(kt):
                    j = kt - 4 * g
                    c0 = 0 if j < 0 else 128 * j
                    buf = kt % 2
                    src = psb[buf][:].rearrange("p (m q) -> p m q", m=2)[:, :, c0:G]
                    dst = pT[:, buf, :].rearrange("p (m q) -> p m q", m=2)[:, :, c0:G]
                    act(dst, src, AF.Exp, [PS(buf * 2), PS(buf * 2 + 1)], [("pT", buf)])

                def emitPV(kt):
                    j = kt - 4 * g
                    c0 = 0 if j < 0 else 128 * j
                    buf = kt % 2
                    for m in range(2):
                        bo, br = 4 + 2 * m, 5 + 2 * m
                        rhs = pT[:, buf, m * G + c0:(m + 1) * G]
                        mm(ps[bo][:, c0:G], vS[:, kt, h * 128:(h + 1) * 128], rhs, kt == 0, kt == nkt - 1,
                           [("vS", kt), ("pT", buf)], [PS(bo)])
                        mm(ps[br][:, c0:G], ones_k[:], rhs, kt == 0, kt == nkt - 1,
                           ["ones_k", ("pT", buf)], [PS(br)])

                emitS(0)
                emitE(0)
                for kt in range(nkt):
                    if kt + 1 < nkt:
                        emitS(kt + 1)
                    if kt >= 1:
                        emitPV(kt - 1)
                        bg_step()
                    if kt + 1 < nkt:
                        emitE(kt + 1)
                emitPV(nkt - 1)
                while finq:
                    bg_step()
                finq.extend(fin_stages(h))
                bg_step()
            while finq or lruq:
                bg_step()
            S.label = "mix.wout"
            for pj in range(4):
                s = wload(wo_s[l, pj], [("wo", l, pj, 0), ("wo", l, pj, 1)])
                for t in range(2):
                    j = pj * 2 + t
                    b = j % 4
                    for k in range(KC):
                        mm(ps[b], wr[s][:, t * 1024 + k * 128:t * 1024 + (k + 1) * 128], mix[:, k, :], k == 0, k == KC - 1,
                           [("w", s), ("mix", k)], [PS(b)])
                    tt(xg[:, j, :], xg[:, j, :], ps[b], ALU.add, [("xg", j), PS(b)], [("xg", j)])

        def load_x0(g):
            for t4 in range(4):
                r0 = g * G + t4 * 128
                bi = t4 % 2
                dma(SP, io[bi], x_in[r0:r0 + 128, :], (), IOT[bi])
                for k in range(KC):
                    S.op(PE, lambda e, k=k, bi=bi, t4=t4: e.transpose(ps[k][:, t4 * 128:(t4 + 1) * 128],
                                                                     io[bi][:, k * 128:(k + 1) * 128], ident[:]),
                         IOT[bi] + ["ident"], [PS(k)])
            for k in range(KC):
                cp(ACT if k % 2 == 0 else DVE, xg[:, k, :], ps[k], [PS(k)], [("xg", k)])

        def final_out(g):
            act(mix[:], xg[:], AF.Square, XG, MIX)
            for k in range(KC):
                mm(ps[0], ones_mean[:], mix[:, k, :], k == 0, k == KC - 1, ["ones_mean", ("mix", k)], [PS(0)])
            rs = tmp[:, 0, :]
            act(rs, ps[0], AF.Sqrt, [PS(0)], [("tmp", 0)], bias=RMS_EPS)
            recip(rs, rs, [("tmp", 0)], [("tmp", 0)])
            for k in range(KC):
                stt(xg[:, k, :], xg[:, k, :], vecs[:, VC["nf"] + k:VC["nf"] + k + 1], rs, ALU.mult, ALU.mult,
                    [("xg", k), "vecs", ("tmp", 0)], [("xg", k)])
            for t4 in range(4):
                bi = t4 % 2
                for half in range(2):
                    b = 1 + ((t4 * 2 + half) % 4)
                    for kk in range(4):
                        k = half * 4 + kk
                        S.op(PE, lambda e, k=k, kk=kk, b=b, t4=t4: e.transpose(ps[b][:, kk * 128:(kk + 1) * 128],
                                                                              xg[:, k, t4 * 128:(t4 + 1) * 128], ident[:]),
                             [("xg", k), "ident"], [PS(b)])
                    cp(ACT if half == 0 else DVE, io[bi][:, half * 512:(half + 1) * 512], ps[b], [PS(b)], [IOT[bi][half]])
                r0 = g * G + t4 * 128
                dma(SP, out[r0:r0 + 128, :], io[bi], IOT[bi], [("out", g, t4)])

        for l in range(nl):
            if l > 0:
                memset(DVE, pbuf[:, 0, 0:2], 0.0, ["pbuf0"])
                memset(DVE, pbuf[:, 1, 0:2], 0.0, ["pbuf1"])
                memset(DVE, lbuf[:, 0, 0:3], 0.0, ["lbuf0"])
                memset(DVE, lbuf[:, 1, 0:3], 0.0, ["lbuf1"])
                memset(DVE, hst[:, 0:1], 0.0, ["hst0"])
                memset(DVE, hst[:, 1:2], 0.0, ["hst1"])
            for g in range(NG):
                nxt = (l, g + 1) if g + 1 < NG else ((l + 1, 0) if l + 1 < nl else None)
                if l == 0:
                    load_x0(g)

                def xs_store(k, g=g):
                    dma(ACT, xs_s[g, :, k, :], xg[:, k, :], [("xg", k)], [("xs", g, k)])

                def xs_load(k, gn):
                    dma(ACT, xg[:, k, :], xs_s[gn, :, k, :], [("xs", gn, k)], [("xg", k)])

                late_loads = []

                def on_evac(k, hh, l=l, g=g, nxt=nxt):
                    fns = []
                    if l < nl - 1:
                        fns.append(lambda: xs_store(k))
                        if nxt is not None and nxt[0] >= 1:
                            fns.append(lambda: xs_load(k, nxt[1]))
                    if not fns:
                        return
                    if hh == 0:
                        deferred.append([2, fns[:1]])
                        if len(fns) > 1:
                            deferred.append([4, fns[1:]])
                    else:
                        fns[0]()
                        late_loads.extend(fns[1:])
                        if k == KC - 1:
                            for fn in late_loads:
                                fn()
                            del late_loads[:]

                S.label = "ffn1"
                ffn(l, 0)
                S.label = "mix"
                mixer(l, g)
                S.label = "ffn2"
                ffn(l, 1, on_evac)
                S.label = "tail"
                if l == nl - 1:
                    final_out(g)
                    if nxt is not None:
                        for k in range(KC):
                            xs_load(k, nxt[1])
        S.op(SP, None, [("out", g, t4) for g in range(NG) for t4 in range(4)], ())
        S.emit()
    nc._sched_stats = S.stats
    nc._pe_labels = [r["label"] for r in S.ops[PE]]
    return nc


def _t5_bucket_np(n):
    n = np.asarray(n)
    max_exact = NBUCK // 2
    nf = np.maximum(n, 1).astype(np.float32)
    large = max_exact + (np.log(nf / np.float32(max_exact)) / np.float32(math.log(128 / max_exact))
                         * np.float32(NBUCK - max_exact)).astype(np.int32)
    large = np.minimum(large, NBUCK - 1)
    return np.where(n < max_exact, n, large)


def _host_consts():
    mbm = np.zeros((NBUCK, 384), np.float32)
    dist = np.arange(256)
    bk = _t5_bucket_np(dist)
    mbm[bk, 128 + dist] = 1.0
    mbm[31, 128:] -= 1.0
    return mbm, np.eye(128, dtype=np.float32)


def _pack_small(inp, nl):
    VC = _vcols(nl)
    v = np.zeros((128, VC["_n"]), np.float32)

    def put(name, arr2d):
        v[:, VC[name]:VC[name] + arr2d.shape[0]] = arr2d.T

    put("n1", np.asarray(inp["ffn1_norm"])[:nl].reshape(nl * 8, 128))
    put("nm", np.asarray(inp["mix_norm"])[:nl].reshape(nl * 8, 128))
    put("n2", np.asarray(inp["ffn2_norm"])[:nl].reshape(nl * 8, 128))
    put("nf", np.asarray(inp["final_norm"]).reshape(8, 128))
    put("scw", np.asarray(inp["sc_conv_w"])[:nl].reshape(nl * 3 * 2, 128))
    put("lcw", np.asarray(inp["lru_conv_w"])[:nl].reshape(nl * 4 * 2, 128))
    put("lcb", np.asarray(inp["lru_conv_b"])[:nl].reshape(nl * 2, 128))
    put("lba", np.asarray(inp["lru_ba"])[:nl].reshape(nl * 2, 128))
    put("lbx", np.asarray(inp["lru_bx"])[:nl].reshape(nl * 2, 128))
    put("llam", np.asarray(inp["lru_lambda"])[:nl].reshape(nl * 2, 128))
    put("subg", np.asarray(inp["subln_gain"])[:nl].reshape(nl, 128))
    lamv = np.stack([np.asarray(inp[k])[:nl] for k in ("lam_q1", "lam_k1", "lam_q2", "lam_k2")], axis=-1)
    lamv = np.ascontiguousarray(lamv.transpose(1, 0, 2).reshape(64, nl * 4)).astype(np.float32)
    wbd = np.zeros((128, nl * 4, 128), np.float32)
    for l in range(nl):
        for ax, nm in enumerate(("lru_wa", "lru_wx")):
            w = np.asarray(inp[nm])[l]
            for cc in range(2):
                for b in range(2):
                    wbd[b * 64:(b + 1) * 64, (l * 2 + ax) * 2 + cc, b * 64:(b + 1) * 64] = w[cc * 2 + b]
    return v, lamv, wbd


_NC_CACHE = {}


def run_model(inp, nl, seq, n_cores, batch_of_core):
    key = (nl, seq)
    if key not in _NC_CACHE:
        _NC_CACHE[key] = build_nc(nl, seq)
    nc = _NC_CACHE[key]
    v, lamv, wbd = _pack_small(inp, nl)
    mbm, ident = _host_consts()
    c = lambda a: np.ascontiguousarray(np.asarray(a, dtype=np.float32))
    shared = dict(
        g1=c(np.asarray(inp["ffn1_gate"])[:nl]), u1=c(np.asarray(inp["ffn1_up"])[:nl]), d1=c(np.asarray(inp["ffn1_down"])[:nl]),
        g2=c(np.asarray(inp["ffn2_gate"])[:nl]), u2=c(np.asarray(inp["ffn2_up"])[:nl]), d2=c(np.asarray(inp["ffn2_down"])[:nl]),
        win=c(np.asarray(inp["w_in"])[:nl]), wout=c(np.asarray(inp["w_out"])[:nl]),
        vecs=v, lamv=lamv, relb=c(inp["rel_bias"]), mb=mbm, ident=ident, wbd=wbd)
    zeros = None
    x = np.asarray(inp["x"], dtype=np.float32)
    in_maps = []
    for ci in range(n_cores):
        if batch_of_core[ci] is None:
            if zeros is None:
                zeros = {k: np.zeros_like(a) for k, a in shared.items()}
                zeros["ident"] = ident
                zeros["mb"] = mbm
                zeros["x"] = np.zeros((seq, D), np.float32)
            in_maps.append(zeros)
        else:
            d = dict(shared)
            d["x"] = np.ascontiguousarray(x[batch_of_core[ci], :seq])
            in_maps.append(d)
    res = run_bass_kernel_spmd(nc, in_maps, core_ids=list(range(n_cores)))
    return res


_WORK_CORES = (0, 1, 4, 5)


def kernel(**inputs):
    x = np.asarray(inputs["x"])
    B, seq, _ = x.shape
    boc = [None] * N_CORES
    for b in range(B):
        boc[_WORK_CORES[b]] = b
    res = run_model(inputs, 4, seq, N_CORES, boc)
    outs = [np.asarray(res.results[_WORK_CORES[b]]["out"], dtype=np.float32) for b in range(B)]
    return np.stack(outs, axis=0)
```

```python
import contextlib
import math
import numpy as np
import concourse.bass as bass
import concourse.mybir as mybir
from concourse.bass_utils import run_bass_kernel_spmd

F32 = mybir.dt.float32
BF16 = mybir.dt.bfloat16
ALU = mybir.AluOpType
AF = mybir.ActivationFunctionType

PE, ACT, DVE, POOL, SP = "pe", "act", "dve", "pool", "sp"
ENGS = (PE, ACT, DVE, POOL, SP)
NDMASEM = 16

D = 1024
KC = 8
DFF = 2816
FC = 22
G = 512
IN_W = 2816
RMS_EPS = 1e-6
SUBLN_EPS = 1e-5
NBUCK = 32
N_CORES = 8
NFILL = 12


class Sched:
    def __init__(self, nc):
        self.nc = nc
        self.ops = {e: [] for e in ENGS}
        self.last_w = {}
        self.readers = {}
        self.dma_cnt = {e: 0 for e in ENGS}
        self.dma_ops = {e: [] for e in ENGS}
        self.label = ""

    def op(self, eng, fn, reads=(), writes=(), dma=False):
        deps = []
        for t in reads:
            w = self.last_w.get(t)
            if w is not None:
                deps.append((w, "raw"))
        for t in writes:
            w = self.last_w.get(t)
            if w is not None:
                deps.append((w, "waw"))
            for r in self.readers.get(t, ()):
                deps.append((r, "war"))
        seq = len(self.ops[eng])
        rec = dict(fn=fn, deps=deps, dma=dma, seq=seq, label=self.label)
        if dma:
            k = self.dma_cnt[eng]
            self.dma_cnt[eng] += 1
            rec["dma_k"] = k
            if k >= NDMASEM:
                deps.append(((eng, self.dma_ops[eng][k - NDMASEM]), "raw"))
            self.dma_ops[eng].append(seq)
        self.ops[eng].append(rec)
        ev = (eng, seq)
        for t in writes:
            self.last_w[t] = ev
            self.readers[t] = []
        for t in reads:
            self.readers.setdefault(t, []).append(ev)
        return ev

    def emit(self):
        nc = self.nc
        for e in ENGS:
            seen = {}
            seen_dma = set()
            for rec in self.ops[e]:
                waits = []
                best = {}
                for (f, s), kind in rec["deps"]:
                    if self.ops[f][s]["dma"]:
                        if (f, s) not in seen_dma:
                            seen_dma.add((f, s))
                            waits.append((f, s, True))
                    else:
                        if f == e and (e == PE or e == SP):
                            continue
                        if seen.get(f, -1) >= s:
                            continue
                        if best.get(f, -1) < s:
                            best[f] = s
                for f, s in best.items():
                    seen[f] = s
                    waits.append((f, s, False))
                rec["waits"] = waits
                rec["deps"] = None
        sig = {e: set() for e in ENGS}
        for e in ENGS:
            for rec in self.ops[e]:
                for (f, s, isdma) in rec["waits"]:
                    if not isdma:
                        sig[f].add(s)
        cnt = {}
        for e in ENGS:
            for i, s in enumerate(sorted(sig[e])):
                cnt[(e, s)] = i + 1
        self.stats = {e: (len(self.ops[e]), len(sig[e])) for e in ENGS}
        with contextlib.ExitStack() as st:
            esem = {e: st.enter_context(nc.semaphore("s_" + e)) for e in ENGS}
            dsem = {e: [st.enter_context(nc.semaphore("d_%s%d" % (e, j))) for j in range(NDMASEM)]
                    for e in ENGS if self.dma_cnt[e] > 0}
            block = st.enter_context(nc.Block())

            def body(e):
                def run(engobj):
                    for rec in self.ops[e]:
                        for (f, s, isdma) in rec["waits"]:
                            if isdma:
                                k = self.ops[f][s]["dma_k"]
                                engobj.wait_ge(dsem[f][k % NDMASEM], 16 * (k // NDMASEM + 1))
                            else:
                                engobj.wait_ge(esem[f], cnt[(f, s)])
                        if rec["fn"] is None:
                            continue
                        ins = rec["fn"](engobj)
                        if rec["dma"]:
                            k = rec["dma_k"]
                            ins.then_inc(dsem[e][k % NDMASEM], 16)
                        elif (e, rec["seq"]) in cnt:
                            ins.then_inc(esem[e], 1)
                return run

            if self.ops[PE]:
                block.tensor(body(PE))
            if self.ops[ACT]:
                block.scalar(body(ACT))
            if self.ops[DVE]:
                block.vector(body(DVE))
            if self.ops[POOL]:
                block.gpsimd(body(POOL))
            if self.ops[SP]:
                block.sync(body(SP))


def _vcols(nl):
    o = {}
    c = 0
    for nm, n in (("n1", nl * 8), ("nm", nl * 8), ("n2", nl * 8), ("nf", 8),
                  ("scw", nl * 6), ("lcw", nl * 8), ("lcb", nl * 2), ("lba", nl * 2),
                  ("lbx", nl * 2), ("llam", nl * 2), ("subg", nl)):
        o[nm] = c
        c += n
    o["_n"] = c
    return o


def _lam_init(l):
    return 0.8 - 0.6 * math.exp(-0.3 * l)


def build_nc(nl=4, seq=4096):
    NG = seq // G
    NT = seq // 128
    VC = _vcols(nl)
    nc = bass.Bass("TRN2", target_bir_lowering=False)

    def din(name, shape, dt=F32):
        return nc.dram_tensor(name, list(shape), dt, kind="ExternalInput").ap()

    x_in = din("x", [seq, D])
    wg = [din("g1", [nl, D, DFF]), din("g2", [nl, D, DFF])]
    wu = [din("u1", [nl, D, DFF]), din("u2", [nl, D, DFF])]
    wd = [din("d1", [nl, DFF, D]), din("d2", [nl, DFF, D])]
    w_in = din("win", [nl, D, IN_W])
    w_out = din("wout", [nl, D, D])
    vecs_in = din("vecs", [128, VC["_n"]])
    lamv_in = din("lamv", [64, nl * 4])
    relb_in = din("relb", [NBUCK, 8])
    mb_in = din("mb", [NBUCK, 384])
    ident_in = din("ident", [128, 128])
    wbd_in = din("wbd", [128, nl * 4, 128])
    out = nc.dram_tensor("out", [seq, D], F32, kind="ExternalOutput").ap()

    def dscr(name, shape, dt):
        return nc.dram_tensor(name, list(shape), dt, kind="Internal").ap()

    gu_s = [dscr("gu_s%d" % f, [nl, FC, 128, 2 * KC * 128], BF16) for f in range(2)]
    dd_s = [dscr("dd_s%d" % f, [nl, 2, 128, FC, 512], BF16) for f in range(2)]
    wi_s = dscr("wi_s", [nl, 9, 128, 2 * KC * 128], BF16)
    wv_s = dscr("wv_s", [nl, 2, 128, KC * 256], BF16)
    wo_s = dscr("wo_s", [nl, 4, 128, 2 * KC * 128], BF16)
    xs_s = dscr("xs_s", [NG, 128, KC, G], F32)
    tz_t = nc.dram_tensor("tz_s", [8, 128 * 385 + 512], F32, kind="Internal")

    S = Sched(nc)
    with contextlib.ExitStack() as st:
        def sb(name, shape, dt):
            return st.enter_context(nc.sbuf_tensor(name + "_sb", list(shape), dt))

        kT = sb("kT", [128, 4, seq], BF16)
        vS = sb("vS", [128, NT, 512], BF16)
        xg = sb("xg", [128, KC, G], F32)
        u = sb("u", [128, KC, G], BF16)
        hid = sb("hid", [128, FC, G], BF16)
        wr = [sb("wr%d" % i, [128, 2048], BF16) for i in range(4)]
        qT = sb("qT", [128, 4, G], BF16)
        mix = sb("mix", [128, KC, G], BF16)
        tmp = sb("tmp", [128, 15, G], F32)
        pbuf = sb("pbuf", [128, 2, 2 + G], F32)
        lbuf = sb("lbuf", [128, 2, 3 + G], F32)
        hst = sb("hst", [128, 2], F32)
        xrb = sb("xrb", [128, 2, G], BF16)
        sqo = sb("sqo", [128, G], BF16)
        pT = sb("pT", [128, 2, 2 * G], BF16)
        vecs = sb("vecs", [128, VC["_n"]], F32)
        dv = sb("dv", [128, 64], F32)
        ident = sb("ident", [128, 128], F32)
        ones_mean = sb("ones_mean", [128, 128], BF16)
        ones_e = sb("ones_e", [128, 128], BF16)
        ones_k = sb("ones_k", [128, 128], BF16)
        ones_f = sb("ones_f", [128, 128], F32)
        wbd = sb("wbd", [128, nl * 4, 128], BF16)
        Bt = sb("Bt", [128, 8, 2, 128], F32)
        relb = sb("relb", [NBUCK, 8], F32)
        mb = sb("mb", [NBUCK, 384], F32)
        rrep = sb("rrep", [NBUCK, 128], F32)
        fb = sb("fb", [128, 384], F32)
        lamv = sb("lamv", [64, nl * 4], F32)
        lprod = sb("lprod", [64, 2, nl], F32)
        psb = [st.enter_context(nc.psum_tensor("psb%d" % b, [128, 1024], F32)) for b in range(4)]
        ps = [psb[b // 2][:, (b % 2) * 512:(b % 2 + 1) * 512] for b in range(8)]
        wbd_f = tmp[:, 0:nl, :].rearrange("p a (b c) -> p (a b) c", c=128)
        WBT = [("tmp", i) for i in range(nl)]
        io = [tmp[:, 6 + 2 * bi:8 + 2 * bi, :].rearrange("p a b -> p (a b)") for bi in range(2)]
        IOT = [[("tmp", 6 + 2 * bi), ("tmp", 7 + 2 * bi)] for bi in range(2)]

        DV_CNEG, DV_CNEG2, DV_LAM, DV_NLAM, DV_GSUB, DV_E = 0, 8, 16, 20, 24, 32

        def PS(b):
            return ("ps", b)

        def dma(eng, out_ap, in_ap, reads, writes):
            S.op(eng, lambda e: e.dma_start(out=out_ap, in_=in_ap), reads, writes, dma=True)

        def mm(out_ap, lhsT, rhs, start, stop, reads, writes):
            S.op(PE, lambda e: e.matmul(out_ap, lhsT, rhs, start=start, stop=stop), reads, writes)

        def act(out_ap, in_ap, func, reads, writes, bias=None, scale=None):
            kw = {}
            if bias is not None:
                kw["bias"] = bias
            if scale is not None:
                kw["scale"] = scale
            S.op(ACT, lambda e: e.activation(out_ap, in_ap, func, **kw), reads, writes)

        def tt(out_ap, a, b, op, reads, writes, eng=DVE):
            S.op(eng, lambda e: e.tensor_tensor(out_ap, a, b, op), reads, writes)

        def ts(out_ap, a, s1, s2, op0, op1, reads, writes, eng=DVE):
            if op1 is None:
                S.op(eng, lambda e: e.tensor_scalar(out_ap, a, s1, None, op0), reads, writes)
            else:
                S.op(eng, lambda e: e.tensor_scalar(out_ap, a, s1, s2, op0, op1), reads, writes)

        def stt(out_ap, a, scalar, b, op0, op1, reads, writes):
            S.op(DVE, lambda e: e.scalar_tensor_tensor(out_ap, a, scalar, b, op0, op1), reads, writes)

        def cp(eng, out_ap, in_ap, reads, writes):
            if eng == ACT:
                act(out_ap, in_ap, AF.Copy, reads, writes)
            else:
                S.op(eng, lambda e: e.tensor_copy(out_ap, in_ap), reads, writes)

        def recip(out_ap, in_ap, reads, writes):
            S.op(DVE, lambda e: e.reciprocal(out_ap, in_ap), reads, writes)

        def memset(eng, ap, val, writes):
            S.op(eng, lambda e: e.memset(ap, val), (), writes)

        def cast_cols(dst, src2d, col0, wtoken):
            dma(POOL, dst.rearrange("p (k j) -> p k j", k=KC),
                src2d[:, col0:col0 + 128].rearrange("(k p) j -> p k j", p=128), (), [wtoken])

        def cast_ffn(l, f):
            for c in range(FC):
                cast_cols(gu_s[f][l, c, :, 0:KC * 128], wg[f][l], c * 128, ("gu", f, l, c, 0))
                cast_cols(gu_s[f][l, c, :, KC * 128:2 * KC * 128], wu[f][l], c * 128, ("gu", f, l, c, 1))
            for hh in range(2):
                for cb in range(0, FC, 4):
                    nb = min(4, FC - cb)
                    dma(POOL, dd_s[f][l, hh, :, cb:cb + nb, :],
                        wd[f][l][cb * 128:(cb + nb) * 128, hh * 512:(hh + 1) * 512].rearrange(
                            "(c p) j -> p c j", p=128), (), [("dd", f, l, hh, cb)])

        FM_COLS = [0, 1, 2, 3, 4, 5, 6, 7, 12, 13, 14, 15, 16, 17, 18, 19, 20, 21]

        def cast_mix(l):
            for pi in range(9):
                for t in range(2):
                    cast_cols(wi_s[l, pi, :, t * 1024:(t + 1) * 1024], w_in[l],
                              FM_COLS[pi * 2 + t] * 128, ("wi", l, pi, t))
            for hv in range(2):
                dma(POOL, wv_s[l, hv].rearrange("p (k j) -> p k j", k=KC),
                    w_in[l][:, 1024 + hv * 256:1024 + (hv + 1) * 256].rearrange("(k p) j -> p k j", p=128),
                    (), [("wv", l, hv)])
            for pj in range(4):
                for t in range(2):
                    cast_cols(wo_s[l, pj, :, t * 1024:(t + 1) * 1024], w_out[l],
                              (pj * 2 + t) * 128, ("wo", l, pj, t))

        for l in range(nl):
            cast_ffn(l, 0)
            cast_mix(l)
            cast_ffn(l, 1)

        dma(SP, vecs[:], vecs_in, (), ["vecs"])
        dma(SP, ident[:], ident_in, (), ["ident"])
        dma(SP, relb[:], relb_in, (), ["relb"])
        dma(SP, mb[:], mb_in, (), ["mb"])
        dma(SP, lamv[:], lamv_in, (), ["lamv"])
        dma(SP, wbd_f, wbd_in, (), WBT)
        memset(DVE, ones_mean[:], 1.0 / D, ["ones_mean"])
        memset(DVE, ones_e[:], 1.0 / 128, ["ones_e"])
        memset(DVE, ones_k[:], 1.0, ["ones_k"])
        memset(DVE, ones_f[:], 1.0, ["ones_f"])
        memset(DVE, fb[:], -1e30, ["fb"])
        memset(DVE, pbuf[:], 0.0, ["pbuf0", "pbuf1"])
        memset(DVE, lbuf[:], 0.0, ["lbuf0", "lbuf1"])
        memset(DVE, hst[:], 0.0, ["hst0", "hst1"])
        cp(DVE, wbd[:], wbd_f, WBT, ["wbd"])

        lv = lamv[:].rearrange("p (l f) -> p l f", f=4)
        tt(lprod[:, 0, :], lv[:, :, 0], lv[:, :, 1], ALU.mult, ["lamv"], ["lprod"])
        tt(lprod[:, 1, :], lv[:, :, 2], lv[:, :, 3], ALU.mult, ["lamv", "lprod"], ["lprod"])
        mm(ps[0][:, 0:2 * nl], ones_f[0:64, :], lprod[:].rearrange("p a l -> p (a l)"), True, True,
           ["ones_f", "lprod"], [PS(0)])
        act(dv[:, DV_E:DV_E + 2 * nl], ps[0][:, 0:2 * nl], AF.Exp, [PS(0)], ["dv_e"])
        tt(dv[:, DV_LAM:DV_LAM + nl], dv[:, DV_E:DV_E + nl], dv[:, DV_E + nl:DV_E + 2 * nl], ALU.subtract,
           ["dv_e"], ["dv_lam"])
        for l in range(nl):
            ts(dv[:, DV_LAM + l:DV_LAM + l + 1], dv[:, DV_LAM + l:DV_LAM + l + 1], _lam_init(l), None, ALU.add, None,
               ["dv_lam"], ["dv_lam"])
            ts(dv[:, DV_GSUB + l:DV_GSUB + l + 1], vecs[:, VC["subg"] + l:VC["subg"] + l + 1], 1.0 - _lam_init(l), None,
               ALU.mult, None, ["vecs"], ["dv_gsub"])
        ts(dv[:, DV_NLAM:DV_NLAM + nl], dv[:, DV_LAM:DV_LAM + nl], -1.0, None, ALU.mult, None, ["dv_lam"], ["dv_nlam"])
        act(dv[:, DV_CNEG:DV_CNEG + 2 * nl], vecs[:, VC["llam"]:VC["llam"] + 2 * nl], AF.Exp, ["vecs"], ["dv_c"], scale=-1.0)
        act(dv[:, DV_CNEG:DV_CNEG + 2 * nl], dv[:, DV_CNEG:DV_CNEG + 2 * nl], AF.Ln, ["dv_c"], ["dv_c"], bias=1.0)
        ts(dv[:, DV_CNEG2:DV_CNEG2 + 2 * nl], dv[:, DV_CNEG:DV_CNEG + 2 * nl], -16.0, None, ALU.mult, None, ["dv_c"], ["dv_c2"])
        ts(dv[:, DV_CNEG:DV_CNEG + 2 * nl], dv[:, DV_CNEG:DV_CNEG + 2 * nl], -8.0, None, ALU.mult, None, ["dv_c", "dv_c2"], ["dv_c"])

        tz_w = [bass.AP(tz_t, hm * (128 * 385 + 512), [[385, 128], [1, 384]]) for hm in range(8)]
        for hm in range(8):
            ts(rrep[:], ones_f[0:NBUCK, :], relb[:, hm:hm + 1], None, ALU.mult, None, ["ones_f", "relb"], ["rrep"])
            mm(ps[1][:, 0:384], rrep[:], mb[:], True, True, ["rrep", "mb"], [PS(1)])
            cp(DVE, fb[:, 128:384], ps[1][:, 128:384], [PS(1)], ["fb"])
            dma(SP, tz_w[hm], fb[:], ["fb"], [("tz", hm)])
            for t, a in enumerate((128, 256)):
                dma(SP, Bt[:, hm, t, :], bass.AP(tz_t, hm * (128 * 385 + 512) + a, [[384, 128], [1, 128]]),
                    [("tz", hm)], [("Bt", hm, t)])

        XG = [("xg", k) for k in range(KC)]
        UU = [("u", k) for k in range(KC)]
        MIX = [("mix", k) for k in range(KC)]
        wslot = [0]

        deferred = []

        def wload(src_ap, rtokens, ncols=2048):
            s = wslot[0] % 4
            wslot[0] += 1
            dma(SP, wr[s][:, 0:ncols], src_ap, rtokens, [("w", s)])
            for item in list(deferred):
                item[0] -= 1
                if item[0] <= 0:
                    deferred.remove(item)
                    for fn in item[1]:
                        fn()
            return s

        def rstd_from(bank, ti, eps):
            rs = tmp[:, ti, :]
            act(rs, ps[bank], AF.Ln, [PS(bank)], [("tmp", ti)], bias=eps)
            act(rs, rs, AF.Exp, [("tmp", ti)], [("tmp", ti)], scale=-0.5)
            return rs

        def norm_to_u(col0, stat_bank, rstd_i):
            for k in range(KC):
                if k % 2 == 0:
                    act(mix[:, k, :], xg[:, k, :], AF.Square, [("xg", k)], [("mix", k)])
                else:
                    tt(mix[:, k, :], xg[:, k, :], xg[:, k, :], ALU.mult, [("xg", k)], [("mix", k)])
                mm(ps[stat_bank], ones_mean[:], mix[:, k, :], k == 0, k == KC - 1, ["ones_mean", ("mix", k)], [PS(stat_bank)])
            rs = rstd_from(stat_bank, rstd_i, RMS_EPS)
            for i in range(NFILL):
                mm(ps[3], ones_mean[:], mix[:, KC - 1, :], True, True, ["ones_mean", ("mix", KC - 1)], [PS(3)])
            for k in range(KC):
                stt(u[:, k, :], xg[:, k, :], vecs[:, col0 + k:col0 + k + 1], rs, ALU.mult, ALU.mult,
                    [("xg", k), "vecs", ("tmp", rstd_i)], [("u", k)])

        def ffn(l, f, on_evac=None):
            lab0 = S.label
            S.label = lab0 + ".norm"
            norm_to_u(VC["n1" if f == 0 else "n2"] + l * 8, 0, 0)
            S.label = lab0 + ".A"
            for c in range(FC):
                s = wload(gu_s[f][l, c], [("gu", f, l, c, 0), ("gu", f, l, c, 1)])
                bg, bu = (0, 1) if c % 2 == 0 else (2, 3)
                for k in range(KC):
                    mm(ps[bg], wr[s][:, k * 128:(k + 1) * 128], u[:, k, :], k == 0, k == KC - 1,
                       [("w", s), ("u", k)], [PS(bg)])
                for k in range(KC):
                    mm(ps[bu], wr[s][:, 1024 + k * 128:1024 + (k + 1) * 128], u[:, k, :], k == 0, k == KC - 1,
                       [("w", s), ("u", k)], [PS(bu)])
                ti = 1 + (c % 2)
                act(tmp[:, ti, :], ps[bg], AF.Silu, [PS(bg)], [("tmp", ti)])
                tt(hid[:, c, :], tmp[:, ti, :], ps[bu], ALU.mult, [("tmp", ti), PS(bu)], [("hid", c)])
            S.label = lab0 + ".B"
            for hh in range(2):
                for cb in range(0, FC, 4):
                    nb = min(4, FC - cb)
                    s = wload(dd_s[f][l, hh, :, cb:cb + nb, :].rearrange("p c j -> p (c j)"),
                              [("dd", f, l, hh, cb)], ncols=nb * 512)
                    for ci in range(nb):
                        c = cb + ci
                        for j in range(4):
                            mm(ps[4 + j], wr[s][:, ci * 512 + j * 128:ci * 512 + (j + 1) * 128], hid[:, c, :],
                               c == 0, c == FC - 1, [("w", s), ("hid", c)], [PS(4 + j)])
                for j in range(4):
                    k = hh * 4 + j
                    stt(xg[:, k, :], ps[4 + j], 0.5, xg[:, k, :], ALU.mult, ALU.add, [PS(4 + j), ("xg", k)], [("xg", k)])
                    if on_evac is not None:
                        on_evac(k, hh)

        def proj_pair(l, pi, banks):
            s = wload(wi_s[l, pi], [("wi", l, pi, 0), ("wi", l, pi, 1)])
            for t in range(2):
                b = banks[t]
                for k in range(KC):
                    mm(ps[b], wr[s][:, t * 1024 + k * 128:t * 1024 + (k + 1) * 128], u[:, k, :], k == 0, k == KC - 1,
                       [("w", s), ("u", k)], [PS(b)])

        def vcol(name, idx):
            c = VC[name] + idx
            return vecs[:, c:c + 1]

        def mixer(l, g):
            S.label = "mix.norm"
            norm_to_u(VC["nm"] + l * 8, 0, 0)
            t0 = g * G
            T = lambda i: tmp[:, i, :]
            TK = lambda i: ("tmp", i)
            LB = lambda cc: "lbuf%d" % cc
            XR = (6, 0)
            RR = (7, 1)
            II = (8, 2)
            AA = (9, 3)
            GG = (4, 5)

            GT = (9, 3)
            S.label = "mix.lru"
            proj_pair(l, 7, (2, 3))
            for cc in range(2):
                cp(ACT, lbuf[:, cc, 3:3 + G], ps[2 + cc], [PS(2 + cc)], [LB(cc)])
            for cc in range(2):
                wc = lambda k: vcol("lcw", (l * 4 + k) * 2 + cc)
                ts(T(XR[cc]), lbuf[:, cc, 3:3 + G], wc(3), vcol("lcb", l * 2 + cc), ALU.mult, ALU.add,
                   [LB(cc), "vecs"], [TK(XR[cc])])
                for k in (2, 1, 0):
                    stt(T(XR[cc]), lbuf[:, cc, k:k + G], wc(k), T(XR[cc]), ALU.mult, ALU.add,
                        [LB(cc), "vecs", TK(XR[cc])], [TK(XR[cc])])
                cp(DVE, lbuf[:, cc, 0:3], lbuf[:, cc, G:G + 3], [LB(cc)], [LB(cc)])
            proj_pair(l, 8, (6, 7))
            for cc in range(2):
                cp(ACT, T(GG[cc]), ps[6 + cc], [PS(6 + cc)], [TK(GG[cc])])
            for cc in range(2):
                gi, sl = GG[cc], GT[cc]
                tt(T(sl), T(gi), T(gi), ALU.mult, [TK(gi)], [TK(sl)])
                ts(T(sl), T(sl), 0.044715, 1.0, ALU.mult, ALU.add, [TK(sl)], [TK(sl)])
                tt(T(sl), T(sl), T(gi), ALU.mult, [TK(sl), TK(gi)], [TK(sl)])
            S.label = "mix.sc"
            proj_pair(l, 4, (4, 5))
            for cc in range(2):
                cp(ACT, T(10 + cc), ps[4 + cc], [PS(4 + cc)], [TK(10 + cc)])
            proj_pair(l, 5, (6, 7))
            for cc in range(2):
                cp(ACT, T(12 + cc), ps[6 + cc], [PS(6 + cc)], [TK(12 + cc)])
            proj_pair(l, 6, (4, 5))
            for cc in range(2):
                pb = "pbuf%d" % cc
                tt(pbuf[:, cc, 2:2 + G], T(12 + cc), ps[4 + cc], ALU.mult, [TK(12 + cc), PS(4 + cc)], [pb])
            for cc in range(2):
                pb = "pbuf%d" % cc
                acc = T(12 + cc)
                wc = lambda k: vcol("scw", (l * 3 + k) * 2 + cc)
                ts(acc, pbuf[:, cc, 2:2 + G], wc(2), None, ALU.mult, None, [pb, "vecs"], [TK(12 + cc)])
                stt(acc, pbuf[:, cc, 1:1 + G], wc(1), acc, ALU.mult, ALU.add, [pb, "vecs", TK(12 + cc)], [TK(12 + cc)])
                stt(acc, pbuf[:, cc, 0:G], wc(0), acc, ALU.mult, ALU.add, [pb, "vecs", TK(12 + cc)], [TK(12 + cc)])
                tt(mix[:, 4 + cc, :], acc, T(10 + cc), ALU.mult, [TK(12 + cc), TK(10 + cc)], [("mix", 4 + cc)])
                cp(DVE, pbuf[:, cc, 0:2], pbuf[:, cc, G:G + 2], [pb], [pb])
            S.label = "mix.qkv"
            for pi in range(2):
                bk = (0, 1) if pi % 2 == 0 else (2, 3)
                proj_pair(l, pi, bk)
                for t in range(2):
                    h = pi * 2 + t
                    act(qT[:, h, :], ps[bk[t]], AF.Copy, [PS(bk[t])], [("qT", h)], scale=0.125)
            S.label = "mix.lru"
            for cc in range(2):
                cp(ACT, xrb[:, cc, :], T(XR[cc]), [TK(XR[cc])], [("xrb", cc)])
            for cc in range(2):
                gi, sl = GG[cc], GT[cc]
                act(T(sl), T(sl), AF.Sigmoid, [TK(sl)], [TK(sl)], scale=2.0 * math.sqrt(2.0 / math.pi))
                tt(T(gi), T(sl), T(gi), ALU.mult, [TK(sl), TK(gi)], [TK(gi)])
            S.label = "mix.qkv"
            for pi in range(2, 4):
                bk = (0, 1) if pi % 2 == 0 else (2, 3)
                proj_pair(l, pi, bk)
                for t in range(2):
                    h = (pi - 2) * 2 + t
                    cp(ACT, kT[:, h, t0:t0 + G], ps[bk[t]], [PS(bk[t])], [("kT", h, g)])
            S.label = "mix.lru"
            for cc in range(2):
                ba, bx = (4, 5) if cc == 0 else (6, 7)
                mm(ps[ba], wbd[:, (l * 2 + 0) * 2 + cc, :], xrb[:, cc, :], True, True, ["wbd", ("xrb", cc)], [PS(ba)])
                mm(ps[bx], wbd[:, (l * 2 + 1) * 2 + cc, :], xrb[:, cc, :], True, True, ["wbd", ("xrb", cc)], [PS(bx)])
                act(T(RR[cc]), ps[ba], AF.Sigmoid, [PS(ba), "vecs"], [TK(RR[cc])], bias=vcol("lba", l * 2 + cc))
                act(T(II[cc]), ps[bx], AF.Sigmoid, [PS(bx), "vecs"], [TK(II[cc])], bias=vcol("lbx", l * 2 + cc))
            S.label = "mix.qkv"
            for hv in range(2):
                s = wload(wv_s[l, hv], [("wv", l, hv)])
                for tt_ in range(4):
                    b = 4 + ((hv * 4 + tt_) % 4)
                    for k in range(KC):
                        mm(ps[b][:, 0:256], u[:, k, tt_ * 128:(tt_ + 1) * 128], wr[s][:, k * 256:(k + 1) * 256],
                           k == 0, k == KC - 1, [("w", s), ("u", k)], [PS(b)])
                    cp(ACT if tt_ % 2 == 0 else DVE, vS[:, g * 4 + tt_, hv * 256:(hv + 1) * 256], ps[b][:, 0:256],
                       [PS(b)], [("vS", g * 4 + tt_)])

            def lru_bg(cc):
                ci = l * 2 + cc
                hs = "hst%d" % cc
                xr, rr, ii, aa, gi = XR[cc], RR[cc], II[cc], AA[cc], GG[cc]
                st = []
                st.append(lambda: act(T(aa), T(rr), AF.Exp, [TK(rr), "dv_c"], [TK(aa)],
                                      scale=dv[:, DV_CNEG + ci:DV_CNEG + ci + 1]))
                st.append(lambda: act(T(rr), T(rr), AF.Exp, [TK(rr), "dv_c2"], [TK(rr)],
                                      scale=dv[:, DV_CNEG2 + ci:DV_CNEG2 + ci + 1]))
                st.append(lambda: tt(T(ii), T(ii), T(xr), ALU.mult, [TK(ii), TK(xr)], [TK(ii)]))
                st.append(lambda: ts(T(rr), T(rr), -1.0, 1.0, ALU.mult, ALU.add, [TK(rr)], [TK(rr)]))
                st.append(lambda: ts(T(rr), T(rr), 1e-18, None, ALU.max, None, [TK(rr)], [TK(rr)]))
                st.append(lambda: act(T(rr), T(rr), AF.Ln, [TK(rr)], [TK(rr)]))
                st.append(lambda: act(T(rr), T(rr), AF.Exp, [TK(rr)], [TK(rr)], scale=0.5))
                st.append(lambda: tt(T(ii), T(ii), T(rr), ALU.mult, [TK(ii), TK(rr)], [TK(ii)]))
                st.append(lambda: S.op(DVE, lambda e: e.tensor_tensor_scan(tmp[:, xr, :], tmp[:, aa, :], tmp[:, ii, :],
                                                                          hst[:, cc:cc + 1], ALU.mult, ALU.add),
                                       [TK(aa), TK(ii), hs], [TK(xr)]))
                st.append(lambda: cp(DVE, hst[:, cc:cc + 1], tmp[:, xr, G - 1:G], [TK(xr)], [hs]))
                st.append(lambda: tt(mix[:, 6 + cc, :], T(gi), T(xr), ALU.mult, [TK(gi), TK(xr)], [("mix", 6 + cc)]))
                return st

            lruq = []
            for a_, b_ in zip(lru_bg(0), lru_bg(1)):
                lruq.append(a_)
                lruq.append(b_)
            finq = []

            def bg_step():
                lab = S.label
                if finq:
                    S.label = "mix.attnfin"
                    finq.pop(0)()
                elif lruq:
                    S.label = "mix.lru"
                    lruq.pop(0)()
                S.label = lab

            nkt = 4 * g + 4

            def fin_stages(h):
                def st0():
                    act(T(10), ps[5], AF.Ln, [PS(5)], [TK(10)])
                    cp(DVE, T(12), ps[4], [PS(4)], [TK(12)])
                    act(T(11), ps[7], AF.Ln, [PS(7)], [TK(11)])
                    cp(DVE, T(13), ps[6], [PS(6)], [TK(13)])

                def st1():
                    act(T(10), T(10), AF.Exp, [TK(10)], [TK(10)], scale=-1.0)
                    act(T(11), T(11), AF.Exp, [TK(11)], [TK(11)], scale=-1.0)

                def st2():
                    tt(T(12), T(12), T(10), ALU.mult, [TK(12), TK(10)], [TK(12)])
                    tt(T(13), T(13), T(11), ALU.mult, [TK(13), TK(11)], [TK(13)])
                    stt(T(12), T(13), dv[:, DV_NLAM + l:DV_NLAM + l + 1], T(12), ALU.mult, ALU.add,
                        [TK(13), TK(12), "dv_nlam"], [TK(12)])

                def st3():
                    act(sqo[:], T(12), AF.Square, [TK(12)], ["sqo"])

                def st4():
                    mm(ps[0], ones_e[:], sqo[:], True, True, ["ones_e", "sqo"], [PS(0)])
                    rstd_from(0, 14, SUBLN_EPS)

                def st5():
                    stt(mix[:, h, :], T(12), dv[:, DV_GSUB + l:DV_GSUB + l + 1], T(14), ALU.mult, ALU.mult,
                        [TK(12), TK(14), "dv_gsub"], [("mix", h)])

                return [st0, st1, st2, st3, st4, st5]

            for h in range(4):
                S.label = "mix.attn"

                def emitS(kt):
                    j = kt - 4 * g
                    c0 = 0 if j < 0 else 128 * j
                    buf = kt % 2
                    for m in range(2):
                        hm = h * 2 + m
                        b = buf * 2 + m
                        mm(ps[b][:, c0:G], kT[m * 64:(m + 1) * 64, h, kt * 128:(kt + 1) * 128], qT[m * 64:(m + 1) * 64, h, c0:G],
                           True, True, [("kT", h, kt // 4), ("qT", h)], [PS(b)])
                    pv = psb[buf][:].rearrange("p (m q) -> p m q", m=2)
                    BTK = [("Bt", h * 2 + m, t) for m in range(2) for t in range(2)]
                    PSK = [PS(buf * 2), PS(buf * 2 + 1)]
                    if j == -1:
                        tt(pv[:, :, 0:128], pv[:, :, 0:128], Bt[:, 2 * h:2 * h + 2, 1, :], ALU.add, PSK + BTK, PSK)
                    elif j == 3:
                        tt(pv[:, :, c0:c0 + 128], pv[:, :, c0:c0 + 128], Bt[:, 2 * h:2 * h + 2, 0, :], ALU.add, PSK + BTK, PSK)
                    elif j >= 0:
                        tt(pv[:, :, c0:c0 + 256], pv[:, :, c0:c0 + 256],
                           Bt[:, 2 * h:2 * h + 2, :, :].rearrange("p m t q -> p m (t q)"), ALU.add, PSK + BTK, PSK)

                def emitE(kt):
                    j = kt - 4 * g
                    c0 = 0 if j < 0 else 128 * j
                    buf = kt % 2
                    src = psb[buf][:].rearrange("p (m q) -> p m q", m=2)[:, :, c0:G]
                    dst = pT[:, buf, :].rearrange("p (m q) -> p m q", m=2)[:, :, c0:G]
                    act(dst, src, AF.Exp, [PS(buf * 2), PS(buf * 2 + 1)], [("pT", buf)])

                def emitPV(kt):
                    j = kt - 4 * g
                    c0 = 0 if j < 0 else 128 * j
                    buf = kt % 2
                    for m in range(2):
                        bo, br = 4 + 2 * m, 5 + 2 * m
                        rhs = pT[:, buf, m * G + c0:(m + 1) * G]
                        mm(ps[bo][:, c0:G], vS[:, kt, h * 128:(h + 1) * 128], rhs, kt == 0, kt == nkt - 1,
                           [("vS", kt), ("pT", buf)], [PS(bo)])
                        mm(ps[br][:, c0:G], ones_k[:], rhs, kt == 0, kt == nkt - 1,
                           ["ones_k", ("pT", buf)], [PS(br)])

                emitS(0)
                emitE(0)
                for kt in range(nkt):
                    if kt + 1 < nkt:
                        emitS(kt + 1)
                    if kt >= 1:
                        emitPV(kt - 1)
                        bg_step()
                    if kt + 1 < nkt:
                        emitE(kt + 1)
                emitPV(nkt - 1)
                while finq:
                    bg_step()
                finq.extend(fin_stages(h))
                bg_step()
            while finq or lruq:
                bg_step()
            S.label = "mix.wout"
            for pj in range(4):
                s = wload(wo_s[l, pj], [("wo", l, pj, 0), ("wo", l, pj, 1)])
                for t in range(2):
                    j = pj * 2 + t
                    b = j % 4
                    for k in range(KC):
                        mm(ps[b], wr[s][:, t * 1024 + k * 128:t * 1024 + (k + 1) * 128], mix[:, k, :], k == 0, k == KC - 1,
                           [("w", s), ("mix", k)], [PS(b)])
                    tt(xg[:, j, :], xg[:, j, :], ps[b], ALU.add, [("xg", j), PS(b)], [("xg", j)])

        def load_x0(g):
            for t4 in range(4):
                r0 = g * G + t4 * 128
                bi = t4 % 2
                dma(SP, io[bi], x_in[r0:r0 + 128, :], (), IOT[bi])
                for k in range(KC):
                    S.op(PE, lambda e, k=k, bi=bi, t4=t4: e.transpose(ps[k][:, t4 * 128:(t4 + 1) * 128],
                                                                     io[bi][:, k * 128:(k + 1) * 128], ident[:]),
                         IOT[bi] + ["ident"], [PS(k)])
            for k in range(KC):
                cp(ACT if k % 2 == 0 else DVE, xg[:, k, :], ps[k], [PS(k)], [("xg", k)])

        def final_out(g):
            act(mix[:], xg[:], AF.Square, XG, MIX)
            for k in range(KC):
                mm(ps[0], ones_mean[:], mix[:, k, :], k == 0, k == KC - 1, ["ones_mean", ("mix", k)], [PS(0)])
            rs = tmp[:, 0, :]
            act(rs, ps[0], AF.Sqrt, [PS(0)], [("tmp", 0)], bias=RMS_EPS)
            recip(rs, rs, [("tmp", 0)], [("tmp", 0)])
            for k in range(KC):
                stt(xg[:, k, :], xg[:, k, :], vecs[:, VC["nf"] + k:VC["nf"] + k + 1], rs, ALU.mult, ALU.mult,
                    [("xg", k), "vecs", ("tmp", 0)], [("xg", k)])
            for t4 in range(4):
                bi = t4 % 2
                for half in range(2):
                    b = 1 + ((t4 * 2 + half) % 4)
                    for kk in range(4):
                        k = half * 4 + kk
                        S.op(PE, lambda e, k=k, kk=kk, b=b, t4=t4: e.transpose(ps[b][:, kk * 128:(kk + 1) * 128],
                                                                              xg[:, k, t4 * 128:(t4 + 1) * 128], ident[:]),
                             [("xg", k), "ident"], [PS(b)])
                    cp(ACT if half == 0 else DVE, io[bi][:, half * 512:(half + 1) * 512], ps[b], [PS(b)], [IOT[bi][half]])
                r0 = g * G + t4 * 128
                dma(SP, out[r0:r0 + 128, :], io[bi], IOT[bi], [("out", g, t4)])

        for l in range(nl):
            if l > 0:
                memset(DVE, pbuf[:, 0, 0:2], 0.0, ["pbuf0"])
                memset(DVE, pbuf[:, 1, 0:2], 0.0, ["pbuf1"])
                memset(DVE, lbuf[:, 0, 0:3], 0.0, ["lbuf0"])
                memset(DVE, lbuf[:, 1, 0:3], 0.0, ["lbuf1"])
                memset(DVE, hst[:, 0:1], 0.0, ["hst0"])
                memset(DVE, hst[:, 1:2], 0.0, ["hst1"])
            for g in range(NG):
                nxt = (l, g + 1) if g + 1 < NG else ((l + 1, 0) if l + 1 < nl else None)
                if l == 0:
                    load_x0(g)

                def xs_store(k, g=g):
                    dma(ACT, xs_s[g, :, k, :], xg[:, k, :], [("xg", k)], [("xs", g, k)])

                def xs_load(k, gn):
                    dma(ACT, xg[:, k, :], xs_s[gn, :, k, :], [("xs", gn, k)], [("xg", k)])

                late_loads = []

                def on_evac(k, hh, l=l, g=g, nxt=nxt):
                    fns = []
                    if l < nl - 1:
                        fns.append(lambda: xs_store(k))
                        if nxt is not None and nxt[0] >= 1:
                            fns.append(lambda: xs_load(k, nxt[1]))
                    if not fns:
                        return
                    if hh == 0:
                        deferred.append([2, fns[:1]])
                        if len(fns) > 1:
                            deferred.append([4, fns[1:]])
                    else:
                        fns[0]()
                        late_loads.extend(fns[1:])
                        if k == KC - 1:
                            for fn in late_loads:
                                fn()
                            del late_loads[:]

                S.label = "ffn1"
                ffn(l, 0)
                S.label = "mix"
                mixer(l, g)
                S.label = "ffn2"
                ffn(l, 1, on_evac)
                S.label = "tail"
                if l == nl - 1:
                    final_out(g)
                    if nxt is not None:
                        for k in range(KC):
                            xs_load(k, nxt[1])
        S.op(SP, None, [("out", g, t4) for g in range(NG) for t4 in range(4)], ())
        S.emit()
    nc._sched_stats = S.stats
    nc._pe_labels = [r["label"] for r in S.ops[PE]]
    return nc


def _t5_bucket_np(n):
    n = np.asarray(n)
    max_exact = NBUCK // 2
    nf = np.maximum(n, 1).astype(np.float32)
    large = max_exact + (np.log(nf / np.float32(max_exact)) / np.float32(math.log(128 / max_exact))
                         * np.float32(NBUCK - max_exact)).astype(np.int32)
    large = np.minimum(large, NBUCK - 1)
    return np.where(n < max_exact, n, large)


def _host_consts():
    mbm = np.zeros((NBUCK, 384), np.float32)
    dist = np.arange(256)
    bk = _t5_bucket_np(dist)
    mbm[bk, 128 + dist] = 1.0
    mbm[31, 128:] -= 1.0
    return mbm, np.eye(128, dtype=np.float32)


def _pack_small(inp, nl):
    VC = _vcols(nl)
    v = np.zeros((128, VC["_n"]), np.float32)

    def put(name, arr2d):
        v[:, VC[name]:VC[name] + arr2d.shape[0]] = arr2d.T

    put("n1", np.asarray(inp["ffn1_norm"])[:nl].reshape(nl * 8, 128))
    put("nm", np.asarray(inp["mix_norm"])[:nl].reshape(nl * 8, 128))
    put("n2", np.asarray(inp["ffn2_norm"])[:nl].reshape(nl * 8, 128))
    put("nf", np.asarray(inp["final_norm"]).reshape(8, 128))
    put("scw", np.asarray(inp["sc_conv_w"])[:nl].reshape(nl * 3 * 2, 128))
    put("lcw", np.asarray(inp["lru_conv_w"])[:nl].reshape(nl * 4 * 2, 128))
    put("lcb", np.asarray(inp["lru_conv_b"])[:nl].reshape(nl * 2, 128))
    put("lba", np.asarray(inp["lru_ba"])[:nl].reshape(nl * 2, 128))
    put("lbx", np.asarray(inp["lru_bx"])[:nl].reshape(nl * 2, 128))
    put("llam", np.asarray(inp["lru_lambda"])[:nl].reshape(nl * 2, 128))
    put("subg", np.asarray(inp["subln_gain"])[:nl].reshape(nl, 128))
    lamv = np.stack([np.asarray(inp[k])[:nl] for k in ("lam_q1", "lam_k1", "lam_q2", "lam_k2")], axis=-1)
    lamv = np.ascontiguousarray(lamv.transpose(1, 0, 2).reshape(64, nl * 4)).astype(np.float32)
    wbd = np.zeros((128, nl * 4, 128), np.float32)
    for l in range(nl):
        for ax, nm in enumerate(("lru_wa", "lru_wx")):
            w = np.asarray(inp[nm])[l]
            for cc in range(2):
                for b in range(2):
                    wbd[b * 64:(b + 1) * 64, (l * 2 + ax) * 2 + cc, b * 64:(b + 1) * 64] = w[cc * 2 + b]
    return v, lamv, wbd


_NC_CACHE = {}


def run_model(inp, nl, seq, n_cores, batch_of_core):
    key = (nl, seq)
    if key not in _NC_CACHE:
        _NC_CACHE[key] = build_nc(nl, seq)
    nc = _NC_CACHE[key]
    v, lamv, wbd = _pack_small(inp, nl)
    mbm, ident = _host_consts()
    c = lambda a: np.ascontiguousarray(np.asarray(a, dtype=np.float32))
    shared = dict(
        g1=c(np.asarray(inp["ffn1_gate"])[:nl]), u1=c(np.asarray(inp["ffn1_up"])[:nl]), d1=c(np.asarray(inp["ffn1_down"])[:nl]),
        g2=c(np.asarray(inp["ffn2_gate"])[:nl]), u2=c(np.asarray(inp["ffn2_up"])[:nl]), d2=c(np.asarray(inp["ffn2_down"])[:nl]),
        win=c(np.asarray(inp["w_in"])[:nl]), wout=c(np.asarray(inp["w_out"])[:nl]),
        vecs=v, lamv=lamv, relb=c(inp["rel_bias"]), mb=mbm, ident=ident, wbd=wbd)
    zeros = None
    x = np.asarray(inp["x"], dtype=np.float32)
    in_maps = []
    for ci in range(n_cores):
        if batch_of_core[ci] is None:
            if zeros is None:
                zeros = {k: np.zeros_like(a) for k, a in shared.items()}
                zeros["ident"] = ident
                zeros["mb"] = mbm
                zeros["x"] = np.zeros((seq, D), np.float32)
            in_maps.append(zeros)
        else:
            d = dict(shared)
            d["x"] = np.ascontiguousarray(x[batch_of_core[ci], :seq])
            in_maps.append(d)
    res = run_bass_kernel_spmd(nc, in_maps, core_ids=list(range(n_cores)))
    return res


_WORK_CORES = (0, 1, 4, 5)


def kernel(**inputs):
    x = np.asarray(inputs["x"])
    B, seq, _ = x.shape
    boc = [None] * N_CORES
    for b in range(B):
        boc[_WORK_CORES[b]] = b
    res = run_model(inputs, 4, seq, N_CORES, boc)
    outs = [np.asarray(res.results[_WORK_CORES[b]]["out"], dtype=np.float32) for b in range(B)]
    return np.stack(outs, axis=0)
```
